# Optimizing a Trainium2 kernel written in Bass

```python
import jax, jax.numpy as jnp
from jax import lax
import numpy as np

D_MODEL = 1024
BATCH = 8
SEQ = 2048
DEPTH = 1
DEC_BATCH = 128
DEC_SEQ = 8
PAST_LEN = 16384
PAGE_SIZE = 128

N_META = 16
D_RNN = D_MODEL
LRU_BLOCKS = 8
LRU_BLOCK = D_RNN // LRU_BLOCKS
CONV_W = 4
LRU_C = 8.0
HG_HEADS = 8
HG_DK = 128
HG_DV = D_MODEL // HG_HEADS
D_HG_K = HG_HEADS * HG_DK
D_HG_V = HG_HEADS * HG_DV
HG_CHUNK = 32
D_FF = 4 * D_MODEL
EPS = 1e-6
COL_WIDTHS = (D_RNN, D_RNN, D_HG_K, D_HG_K, D_HG_V, D_HG_V, D_MODEL, D_MODEL)
D_IN = D_RNN * 2 + D_HG_K * 2 + D_HG_V * 2 + D_MODEL * 2

kernel_name = 'hybrid_rglru_hgrn2_meta_decode_step'


def rms_norm(x, g):
    xf = x.astype(jnp.float32)
    y = xf * lax.rsqrt(jnp.mean(xf * xf, axis=-1, keepdims=True) + EPS)
    return (y * g.astype(jnp.float32)).astype(x.dtype)


def causal_conv(u, buf, w, b):
    L = u.shape[1]
    up = jnp.concatenate([buf.astype(u.dtype), u], axis=1)
    out = b + w[0] * up[:, 0:L]
    for k in range(1, CONV_W):
        out = out + w[k] * up[:, k:k + L]
    return out, up[:, L:]


def block_diag(u, w, b):
    ub = u.reshape(u.shape[:-1] + (LRU_BLOCKS, LRU_BLOCK))
    return jnp.einsum('blhi,hij->blhj', ub, w.astype(jnp.float32)).reshape(u.shape) + b.astype(jnp.float32)


def rg_lru(u, h0, w_r, b_r, w_i, b_i, lam, reset_first):
    uf = u.astype(jnp.float32)
    r = jax.nn.sigmoid(block_diag(uf, w_r, b_r))
    ig = jax.nn.sigmoid(block_diag(uf, w_i, b_i))
    log_a = -LRU_C * r * jax.nn.softplus(-lam.astype(jnp.float32))
    a = jnp.exp(log_a)
    mult = jnp.sqrt(-jnp.expm1(2.0 * log_a))
    if reset_first:
        mult = mult.at[:, 0].set(1.0)
    bterm = mult * ig * uf

    def combine(c1, c2):
        a1, b1 = c1
        a2, b2 = c2
        return a1 * a2, a2 * b1 + b2

    a_cum, b_cum = lax.associative_scan(combine, (a, bterm), axis=1)
    h = a_cum * h0.astype(jnp.float32)[:, None] + b_cum
    return h, h[:, -1]


def hgrn2_chunk(S, q, k, v, logf):
    C = q.shape[1]
    b = jnp.cumsum(logf, axis=1)
    o_inter = jnp.einsum('bthk,bhkv->bthv', q * jnp.exp(b), S)
    causal = jnp.tril(jnp.ones((C, C), dtype=bool))
    diff = b[:, :, None] - b[:, None, :]
    decay = jnp.where(causal[None, :, :, None, None], jnp.exp(jnp.minimum(diff, 0.0)), 0.0)
    A = jnp.einsum('bthk,btshk,bshk->bhts', q, decay, k)
    o_intra = jnp.einsum('bhts,bshv->bthv', A, v)
    b_last = b[:, -1]
    S_new = jnp.exp(b_last)[..., None] * S + jnp.einsum('bshk,bshv->bhkv', k * jnp.exp(b_last[:, None] - b), v)
    return S_new, o_inter + o_intra


def hgrn2(q, f, i, S0, lb, meta_lead):
    Bn, L, _ = q.shape
    qf = jax.nn.silu(q.astype(jnp.float32)).reshape(Bn, L, HG_HEADS, HG_DK)
    fg = lb + (1.0 - lb) * jax.nn.sigmoid(f.astype(jnp.float32))
    logf = jnp.log(fg).reshape(Bn, L, HG_HEADS, HG_DK)
    k = (1.0 - fg).reshape(Bn, L, HG_HEADS, HG_DK)
    v = i.astype(jnp.float32).reshape(Bn, L, HG_HEADS, HG_DV)
    S0 = S0.astype(jnp.float32)
    if meta_lead:
        S, o_meta = hgrn2_chunk(S0, qf[:, :N_META], k[:, :N_META], v[:, :N_META], logf[:, :N_META])
        rest = L - N_META
        n_chunks = rest // HG_CHUNK

        def to_chunks(t):
            return t[:, N_META:].reshape((Bn, n_chunks, HG_CHUNK) + t.shape[2:]).swapaxes(0, 1)

        def step(S_c, xs):
            return hgrn2_chunk(S_c, *xs)

        S, o_rest = lax.scan(step, S, (to_chunks(qf), to_chunks(k), to_chunks(v), to_chunks(logf)))
        o_rest = o_rest.swapaxes(0, 1).reshape(Bn, rest, HG_HEADS, HG_DV)
        o = jnp.concatenate([o_meta, o_rest], axis=1)
    else:
        S, o = hgrn2_chunk(S0, qf, k, v, logf)
    return o, S


def layer(x, conv_buf, h0, S0, is_prompt, lb, gains, w_in, conv_w, conv_b, rg_w, rg_b, ig_w, ig_b,
          lru_lambda, hg_gnorm, w_branch_a, w_branch_b, w_out, w_up, w_down):
    Bn, L, _ = x.shape
    xn = rms_norm(x, gains[0])
    proj = xn @ w_in
    split_idx = np.cumsum(np.array(COL_WIDTHS))[:-1].tolist()
    u, gate_a, q, f, i, og, m_a, m_b = jnp.split(proj, split_idx, axis=-1)
    uc, conv_new = causal_conv(u, conv_buf, conv_w, conv_b)
    h, h_last = rg_lru(uc, h0, rg_w, rg_b, ig_w, ig_b, lru_lambda, is_prompt)
    y_a = h.astype(x.dtype) * jax.nn.gelu(gate_a)
    o, S_new = hgrn2(q, f, i, S0, lb, is_prompt)
    o = rms_norm(o, hg_gnorm.reshape(HG_HEADS, HG_DV)).reshape(Bn, L, D_HG_V)
    y_b = o.astype(x.dtype) * jax.nn.silu(og)
    mixed = jax.nn.sigmoid(m_a) * (y_a @ w_branch_a) + jax.nn.sigmoid(m_b) * (y_b @ w_branch_b)
    x = x + rms_norm(mixed @ w_out, gains[1])
    hn = rms_norm(x, gains[2])
    ff = jnp.square(jax.nn.relu(hn @ w_up)) @ w_down
    x = x + rms_norm(ff, gains[3])
    return x, conv_new, h_last, S_new


def setup_inputs(seed: int = 0) -> dict:
    key = jax.random.key(seed)
    ks = jax.random.split(key, 24)
    nrm = jax.random.normal
    f32 = jnp.float32
    u = jax.random.uniform(ks[13], (DEPTH, D_RNN), f32, 0.9, 0.999) ** (1.0 / LRU_C)
    return {
        'x_prompt': nrm(ks[0], (BATCH, SEQ, D_MODEL), f32),
        'x_sample': nrm(ks[1], (DEC_BATCH, DEC_SEQ, D_MODEL), f32),
        'state_conv': nrm(ks[2], (DEPTH, DEC_BATCH, CONV_W - 1, D_RNN), f32),
        'state_rglru': nrm(ks[3], (DEPTH, DEC_BATCH, D_RNN), f32),
        'state_hgrn': 0.5 * nrm(ks[4], (DEPTH, DEC_BATCH, HG_HEADS, HG_DK, HG_DV), f32),
        'meta_tokens': nrm(ks[5], (N_META, D_MODEL), f32),
        'norm_gains': 1.0 + 0.05 * nrm(ks[6], (DEPTH, 4, D_MODEL), f32),
        'w_in': nrm(ks[7], (DEPTH, D_MODEL, D_IN), f32) * D_MODEL ** -0.5,
        'conv_w': nrm(ks[8], (DEPTH, CONV_W, D_RNN), f32) * CONV_W ** -0.5,
        'conv_b': 0.02 * nrm(ks[9], (DEPTH, D_RNN), f32),
        'rg_w': nrm(ks[10], (DEPTH, LRU_BLOCKS, LRU_BLOCK, LRU_BLOCK), f32) * LRU_BLOCK ** -0.5,
        'rg_b': 0.02 * nrm(ks[11], (DEPTH, D_RNN), f32),
        'ig_w': nrm(ks[12], (DEPTH, LRU_BLOCKS, LRU_BLOCK, LRU_BLOCK), f32) * LRU_BLOCK ** -0.5,
        'ig_b': 0.02 * nrm(ks[14], (DEPTH, D_RNN), f32),
        'lru_lambda': jnp.log(u) - jnp.log1p(-u),
        'hgrn_lb': 0.5 * nrm(ks[15], (DEPTH + 1, D_HG_K), f32),
        'hgrn_gnorm': 1.0 + 0.05 * nrm(ks[16], (DEPTH, D_HG_V), f32),
        'w_branch_a': nrm(ks[17], (DEPTH, D_RNN, D_MODEL), f32) * D_RNN ** -0.5,
        'w_branch_b': nrm(ks[18], (DEPTH, D_HG_V, D_MODEL), f32) * D_HG_V ** -0.5,
        'w_out': nrm(ks[19], (DEPTH, D_MODEL, D_MODEL), f32) * D_MODEL ** -0.5,
        'w_up': nrm(ks[20], (DEPTH, D_MODEL, D_FF), f32) * D_MODEL ** -0.5,
        'w_down': nrm(ks[21], (DEPTH, D_FF, D_MODEL), f32) * D_FF ** -0.5,
    }


def reference(x_prompt, x_sample, state_conv, state_rglru, state_hgrn, meta_tokens, norm_gains, w_in,
              conv_w, conv_b, rg_w, rg_b, ig_w, ig_b, lru_lambda, hgrn_lb, hgrn_gnorm,
              w_branch_a, w_branch_b, w_out, w_up, w_down):
    bp = x_prompt.shape[0]
    meta = jnp.broadcast_to(meta_tokens.astype(x_prompt.dtype)[None], (bp, N_META, D_MODEL))
    xp = jnp.concatenate([meta, x_prompt], axis=1)
    xs = x_sample
    lb_all = jnp.cumsum(jax.nn.softmax(hgrn_lb.astype(jnp.float32), axis=0), axis=0)
    conv_p, h_p, S_p, conv_s, h_s, S_s = [], [], [], [], [], []
    for l in range(DEPTH):
        params = (norm_gains[l], w_in[l], conv_w[l], conv_b[l], rg_w[l], rg_b[l], ig_w[l], ig_b[l],
                  lru_lambda[l], hgrn_gnorm[l], w_branch_a[l], w_branch_b[l], w_out[l], w_up[l], w_down[l])
        zc = jnp.zeros((bp, CONV_W - 1, D_RNN), xp.dtype)
        zh = jnp.zeros((bp, D_RNN), jnp.float32)
        zS = jnp.zeros((bp, HG_HEADS, HG_DK, HG_DV), jnp.float32)
        xp, c1, h1, s1 = layer(xp, zc, zh, zS, True, lb_all[l], *params)
        xs, c2, h2, s2 = layer(xs, state_conv[l], state_rglru[l], state_hgrn[l], False, lb_all[l], *params)
        conv_p.append(c1); h_p.append(h1); S_p.append(s1)
        conv_s.append(c2); h_s.append(h2); S_s.append(s2)
    y_prompt = xp[:, N_META:]
    y_sample = xs
    return (y_prompt, y_sample, jnp.stack(conv_p), jnp.stack(h_p), jnp.stack(S_p),
            jnp.stack(conv_s), jnp.stack(h_s), jnp.stack(S_s))
```

```python
import contextlib
import numpy as np
import concourse.bass as bass
import concourse.mybir as mybir
from concourse.bass_utils import run_bass_kernel_spmd

F32 = mybir.dt.float32
BF16 = mybir.dt.bfloat16
AF = mybir.ActivationFunctionType
ALU = mybir.AluOpType

D = 1024
KC = 8
SEQ = 2048
NMETA = 16
NS = 16
DEC = 8
EPS = 1e-6
NPRM = 20
STRICT_SAME_ENGINE = True


class Sched:
    def __init__(self, nc):
        self.nc = nc
        self.ops = []
        self.last_write = {}
        self.readers = {}
        self.dma_slot_count = {}

    def op(self, eng, fn, reads=(), writes=(), dma_slot=None):
        deps = set()
        for k in reads:
            j = self.last_write.get(k)
            if j is not None:
                deps.add((j, 'raw'))
        for k in writes:
            j = self.last_write.get(k)
            if j is not None:
                deps.add((j, 'waw'))
            for r in self.readers.get(k, ()):
                deps.add((r, 'war'))
        idx = len(self.ops)
        rec = dict(eng=eng, fn=fn, deps=deps, dma_slot=dma_slot, signaled=False, seq=None)
        if dma_slot is not None:
            c = self.dma_slot_count.get(dma_slot, 0) + 16
            self.dma_slot_count[dma_slot] = c
            rec['dma_val'] = c
        self.ops.append(rec)
        for k in reads:
            self.readers.setdefault(k, []).append(idx)
        for k in writes:
            self.last_write[k] = idx
            self.readers[k] = []
        return idx

    def emit(self, final_wait_eng='sp'):
        nc = self.nc
        ops = self.ops
        for i, o in enumerate(ops):
            nd = {}
            for (j, kind) in o['deps']:
                p = ops[j]
                if j == i:
                    continue
                if p['dma_slot'] is None and p['eng'] == o['eng'] and o['dma_slot'] is None:
                    if o['eng'] == 'pe' or (kind != 'raw' and not STRICT_SAME_ENGINE):
                        continue
                if kind == 'raw' or j not in nd:
                    nd[j] = kind
            o['deps'] = nd
            for j in nd:
                if ops[j]['dma_slot'] is None:
                    ops[j]['signaled'] = True
        engs = ['pe', 'act', 'dve', 'pool', 'sp']
        cnt = {e: 0 for e in engs}
        for o in ops:
            if o['dma_slot'] is None and o['signaled']:
                cnt[o['eng']] += 1
                o['seq'] = cnt[o['eng']]
        with contextlib.ExitStack() as st:
            sems = {e: st.enter_context(nc.semaphore('s_' + e)) for e in engs}
            dsems = {k: st.enter_context(nc.semaphore('d_' + str(k))) for k in self.dma_slot_count}
            block = st.enter_context(nc.Block())
            per_eng = {e: [] for e in engs}
            for i, o in enumerate(ops):
                per_eng[o['eng']].append(i)

            def make(e):
                def body(eng):
                    waited = {}
                    for i in per_eng[e]:
                        o = ops[i]
                        need = {}
                        for j in sorted(o['deps']):
                            p = ops[j]
                            if p['dma_slot'] is not None:
                                s, v = dsems[p['dma_slot']], p['dma_val']
                            else:
                                s, v = sems[p['eng']], p['seq']
                            if v > need.get(id(s), (None, 0))[1]:
                                need[id(s)] = (s, v)
                        for key, (s, v) in need.items():
                            if waited.get(key, 0) >= v:
                                continue
                            waited[key] = v
                            eng.wait_ge(s, v)
                        ins = o['fn'](eng)
                        if o['dma_slot'] is not None:
                            ins.then_inc(dsems[o['dma_slot']], 16)
                        elif o['signaled']:
                            ins.then_inc(sems[e], 1)
                    if e == final_wait_eng:
                        for k, c in self.dma_slot_count.items():
                            if waited.get(id(dsems[k]), 0) < c:
                                eng.wait_ge(dsems[k], c)
                return body

            block.tensor(make('pe'))
            block.scalar(make('act'))
            block.vector(make('dve'))
            block.gpsimd(make('pool'))
            block.sync(make('sp'))
        return cnt


def make_consts():
    c = np.zeros((128, 1184), np.float32)
    p = np.arange(128)
    c[:, 0:128] = np.eye(128)
    same32 = (p[:, None] // 32) == (p[None, :] // 32)
    same8 = (p[:, None] // 8) == (p[None, :] // 8)
    le = p[:, None] <= p[None, :]
    c[:, 128:256] = (same32 & le)
    c[:, 256:384] = (same8 & le)
    c[:, 384:400] = (p[:, None] // 8) == np.arange(16)[None, :]
    t = np.arange(512)
    c[:, 400:912] = (t % 32 != 0)[None, :]
    tt = np.arange(144)
    c[:, 912:1056] = (~((tt == 0) | ((tt >= 16) & ((tt - 16) % 8 == 0))))[None, :]
    c[:, 1056:1184] = 1.0
    return c


def build_nc():
    nc = bass.Bass("TRN2", target_bir_lowering=False)
    din = lambda n, s: nc.dram_tensor(n, s, F32, kind="ExternalInput").ap()
    dout = lambda n, s: nc.dram_tensor(n, s, F32, kind="ExternalOutput").ap()
    xp = din("xp", [SEQ, D]); xs = din("xs", [NS * DEC, D])
    sconv = din("sconv", [NS * 3, D]); sh = din("sh", [NS, D]); shg = din("shg", [NS, 8, 128, 128])
    meta = din("meta", [NMETA, D]); gains = din("gains", [4, D])
    w_in = din("w_in", [D, 8 * D]); conv_w = din("conv_w", [4, D]); conv_b = din("conv_b", [1, D])
    rg_w = din("rg_w", [8, 128, 128]); rg_b = din("rg_b", [1, D])
    ig_w = din("ig_w", [8, 128, 128]); ig_b = din("ig_b", [1, D])
    lam = din("lam", [1, D]); hlb = din("hlb", [2, D]); gnorm = din("gnorm", [1, D])
    w_a = din("w_a", [D, D]); w_b = din("w_b", [D, D]); w_out = din("w_out", [D, D])
    w_up = din("w_up", [D, 4 * D]); w_down = din("w_down", [4 * D, D])
    consts = din("consts", [128, 1184])
    prows = din("prows", [16, D])
    yp = dout("yp", [SEQ, D]); ys = dout("ys", [NS * DEC, D])
    cp = dout("cp", [3, D]); hp = dout("hp", [1, D]); Sp = dout("Sp", [8, 128, 128])
    cs = dout("cs", [NS * 3, D]); hs = dout("hs", [NS, D]); Ss = dout("Ss", [NS, 8, 128, 128])

    with contextlib.ExitStack() as st:
        SB = lambda name, shape, dt: st.enter_context(nc.sbuf_tensor(name, shape, dt))
        PS = lambda name, shape, dt: st.enter_context(nc.psum_tensor(name, shape, dt))
        S = Sched(nc)

        WB = SB("WB", [128, 4 * 4096], F32)
        W = [WB[:, i * 4096:(i + 1) * 4096].rearrange("p (c t) -> p c t", c=8) for i in range(4)]
        zT = WB[:, 0:4096].rearrange("p (s d) -> p s d", s=4)
        HB = [SB("H%d" % i, [128, 8, 512], BF16) for i in range(3)]
        HU = SB("HU", [128, 8, 516], F32)
        HUflat = HU[:, :, :].rearrange("p c t -> p (c t)")
        H3 = HUflat[:, 0:2048].bitcast(BF16).rearrange("p (c t) -> p c t", c=8)
        H4 = HUflat[:, 2048:4096].bitcast(BF16).rearrange("p (c t) -> p c t", c=8)
        uF = HU
        XN = HUflat[:, 0:4096].rearrange("p (s d) -> p s d", s=4)
        ucar = SB("ucar", [128, 8, 3], F32)
        FFb = WB[:, 4096:12288].bitcast(BF16).rearrange("p (f t) -> p f t", f=32)
        xT = SB("xT", [128, 4, 1024], F32)
        xnb = SB("xnb", [128, 4, 1024], BF16)
        vT = SB("vT", [128, 4, 1024], BF16)
        kT = SB("kT", [128, 4, 1024], BF16)
        slabs = [SB("slab%d" % i, [128, 8, 512], BF16) for i in range(3)]
        rgwb = SB("rgwb", [128, 8, 128], BF16)
        igwb = SB("igwb", [128, 8, 128], BF16)
        g1rep = SB("g1rep", [128, 1024], F32)
        g3rep = SB("g3rep", [128, 1024], F32)
        cst = SB("cst", [128, 1184], F32)
        identb = SB("identb", [128, 128], BF16)
        onesb = SB("onesb", [128, 128], BF16)
        io68 = SB("io68", [68, 1024], F32)
        prow = io68
        sin = io68
        outT = io68
        prm = SB("prm", [128, 8, NPRM], F32)
        hc = SB("hc", [128, 8], F32)
        ebuf = SB("ebuf", [128, 8, 17], F32)
        Sst = SB("Sst", [128, 8, 128], F32)
        Sbf = SB("Sbf", [128, 8, 128], BF16)
        ATms = [SB("ATm", [128, 8, 128], BF16), SB("ATm2", [128, 8, 128], BF16)]
        stat = SB("stat", [128, 16], F32)
        sqt = [W[3][:, i, :] for i in range(2)]
        upS = SB("upS", [128, 8, 16, 11], F32)
        v23 = vT[:, 2:4, :].rearrange("p s d -> p (s d)").bitcast(F32)
        stage = v23[:, 0:544].rearrange("p (c i) -> p c i", c=8)
        h0s = v23[:, 544:672].rearrange("p (c i) -> p c i", c=8)
        tmpS = v23[:, 672:800].rearrange("p (c i) -> p c i", c=8)
        S0f = [W[3][:, :, 256:384], W[3][:, :, 384:512],
               HB[1][:, :, 256:512].bitcast(F32), HB[2][:, :, 256:512].bitcast(F32), HB[0][:, :, 256:512].bitcast(F32),
               xT[:, 2, :].rearrange("p (h v) -> p h v", h=8), xT[:, 3, :].rearrange("p (h v) -> p h v", h=8)]
        NR = len(S0f)
        S0b = [xnb[:, 2 + i, :].rearrange("p (h v) -> p h v", h=8) for i in range(2)]
        kTm = [kT[:, 2 + i, :] for i in range(2)]
        banks = [PS("pb%d" % i, [128, 512], F32) for i in range(7)]
        psT = PS("psT", [128, 1024], BF16)
        banks.append(psT[:, :].bitcast(F32))

        identf = cst[:, 0:128]
        mask64 = cst[:, 128:256]
        mask8 = cst[:, 256:384]
        rowmask = cst[:, 384:400]
        NSMp = cst[:, 400:912]
        NSMt = cst[:, 912:1056]
        H0, H1, H2 = HB

        bank_rr = [0]

        def nb(pool=(0, 1, 2, 3, 4, 5, 6)):
            i = pool[bank_rr[0] % len(pool)]
            bank_rr[0] += 1
            return i

        def bk(i):
            return 'psT' if i == 7 else 'pb%d' % i

        def MM(out, lhsT, rhs, start, stop, reads, writes, **kw):
            S.op('pe', lambda e: e.matmul(out=out, lhsT=lhsT, rhs=rhs, start=start, stop=stop, **kw), reads, writes)

        def TR(out, in_, ident, reads, writes):
            S.op('pe', lambda e: e.transpose(out=out, in_=in_, identity=ident), reads, writes)

        def ACT(out, in_, func, reads, writes, **kw):
            S.op('act', lambda e: e.activation(out=out, in_=in_, func=func, **kw), reads, writes)

        def TT(out, in0, in1, op, reads, writes, eng='dve'):
            S.op(eng, lambda e: e.tensor_tensor(out=out, in0=in0, in1=in1, op=op), reads, writes)

        def TS(out, in0, s1, s2, op0, op1, reads, writes, eng='dve'):
            if s2 is None:
                S.op(eng, lambda e: e.tensor_scalar(out=out, in0=in0, scalar1=s1, scalar2=None, op0=op0), reads, writes)
            else:
                S.op(eng, lambda e: e.tensor_scalar(out=out, in0=in0, scalar1=s1, scalar2=s2, op0=op0, op1=op1), reads, writes)

        def STT(out, in0, scalar, in1, op0, op1, reads, writes, eng='dve'):
            S.op(eng, lambda e: e.scalar_tensor_tensor(out=out, in0=in0, scalar=scalar, in1=in1, op0=op0, op1=op1), reads, writes)

        def CP(out, in_, reads, writes, eng='dve'):
            S.op(eng, lambda e: e.tensor_copy(out=out, in_=in_), reads, writes)

        def SCAN(out, d0, d1, init, op0, op1, reads, writes):
            S.op('dve', lambda e: e.tensor_tensor_scan(out=out, data0=d0, data1=d1, initial=init, op0=op0, op1=op1), reads, writes)

        def RECIP(out, in_, reads, writes):
            S.op('dve', lambda e: e.reciprocal(out=out, in_=in_), reads, writes)

        def MEMSET(ap, val, reads, writes, eng='dve'):
            S.op(eng, lambda e: e.memset(ap, val), reads, writes)

        def DMA(q, out, in_, reads, writes, slot, **kw):
            S.op(q, lambda e: e.dma_start(out=out, in_=in_, **kw), reads, writes, dma_slot=slot)

        def kk(name, idx=range(8)):
            return ['%s.%d' % (name, i) for i in idx]

        HUK = kk('H3') + kk('H4')
        VK2 = ['vT.2.0', 'vT.2.1', 'vT.3.0', 'vT.3.1']

        DMA('sp', xT[0:16, 0, :], meta, [], ['xT.0m'], 'xT.0m')
        DMA('sp', xT[16:128, 0, :], xp[0:112, :], [], ['xT.0'], 'xT.0')
        for s_ in range(1, 4):
            DMA('sp', xT[:, s_, :], xp[128 * s_ - NMETA:128 * s_ - NMETA + 128, :], [], ['xT.%d' % s_], 'xT.%d' % s_)
        DMA('sp', cst[:], consts, [], ['cst'], 'cst')
        DMA('pool', identb[:], consts[:, 0:128], [], ['identb'], 'identb')
        DMA('pool', onesb[:], consts[:, 1056:1184], [], ['onesb'], 'onesb')
        DMA('pool', rgwb[:], rg_w.rearrange("h i j -> i h j"), [], ['rgwb'], 'rgwb')
        DMA('pool', igwb[:], ig_w.rearrange("h i j -> i h j"), [], ['igwb'], 'igwb')
        DMA('sp', g1rep[:], gains[1, :].partition_broadcast(128), [], ['g1rep'], 'g1rep')
        DMA('sp', g3rep[:], gains[3, :].partition_broadcast(128), [], ['g3rep'], 'g3rep')
        DMA('sp', prow[0:16, :], prows, [], ['io68'], 'io68')
        pbi = nb()
        for c in range(8):
            TR(banks[pbi][:, c * 16:c * 16 + 16], prow[0:16, c * 128:(c + 1) * 128], identf[0:16, 0:16],
               ['io68', 'cst'], [bk(pbi)])
        MEMSET(prm[:], 0.0, [], ['prm'])
        CP(prm[:, :, 0:16], banks[pbi][:, 0:128].rearrange("p (c i) -> p c i", c=8), [bk(pbi)], ['prm'])
        P_G0, P_G2, P_CW, P_CB, P_RB, P_IB, P_LAM, P_L0, P_L1, P_GN = 0, 1, 2, 6, 7, 8, 9, 10, 11, 12
        P_CL, P_CL2, P_LB, P_C1, P_T = 13, 14, 15, 16, 17

        def pr(i, c=None):
            if c is None:
                return prm[:, :, i:i + 1]
            return prm[:, c, i:i + 1]

        ACT(pr(P_T), pr(P_LAM), AF.Exp, ['prm'], ['prm'], scale=-1.0)
        ACT(pr(P_T), pr(P_T), AF.Ln, ['prm'], ['prm'], bias=1.0)
        TS(pr(P_CL), pr(P_T), -8.0, None, ALU.mult, None, ['prm'], ['prm'])
        TS(pr(P_CL2), pr(P_T), -16.0, None, ALU.mult, None, ['prm'], ['prm'])
        TT(pr(P_T), pr(P_L0), pr(P_L1), ALU.subtract, ['prm'], ['prm'])
        ACT(pr(P_LB), pr(P_T), AF.Sigmoid, ['prm'], ['prm'])
        TS(pr(P_C1), pr(P_LB), -1.0, 1.0, ALU.mult, ALU.add, ['prm'], ['prm'])
        MEMSET(hc[:], 0.0, [], ['hc'])
        MEMSET(Sst[:], 0.0, [], kk('Sst'))
        MEMSET(Sbf[:], 0.0, [], kk('Sbf'))

        slab_rr = [0]

        NSLAB_PASS = 38
        wscr = nc.dram_tensor("wscr", [NSLAB_PASS, 128, 8 * 512], BF16).ap()
        cur_pass = [0]
        slab_k = [0]

        def load_slab(src_ap):
            i = slab_rr[0] % 3
            slab_rr[0] += 1
            k = slab_k[0]
            slab_k[0] += 1
            key = 'slab%d' % i
            scr = wscr[k].rearrange("p (c n) -> p c n", c=8)
            if cur_pass[0] == 4:
                DMA('pool', slabs[i][:], scr, ['scr.%d' % k], [key], key)
            else:
                DMA('pool', slabs[i][:], src_ap, [], [key], key)
                if cur_pass[0] == 3:
                    DMA('sp', scr, slabs[i][:], [key], ['scr.%d' % k], 'scr.%d' % k)
            return slabs[i], key

        def wslab(wmat, row0, col0):
            return wmat[row0:row0 + 1024, col0:col0 + 512].rearrange("(kc p) n -> p kc n", p=128)

        S0_EXTRA = [kk('W3'), kk('W3'), kk('H1'), kk('H2'), kk('H0'), ['xT.2'], ['xT.3']]
        s0_seen = set()

        def s0_load(i, after=()):
            q = i % NR
            extra = [] if q in s0_seen else S0_EXTRA[q]
            s0_seen.add(q)
            DMA('sp', S0f[q][:], shg[i].rearrange("h k v -> k h v"), list(after), ['S0r.%d' % q] + extra, 'S0r.%d' % q)

        def run(tasks):
            for t in tasks:
                t()

        for ps_i in range(5):
            last = ps_i == 4
            assert ps_i == 0 or slab_k[0] == NSLAB_PASS, slab_k[0]
            cur_pass[0] = ps_i
            slab_k[0] = 0
            T = 144 if last else 512
            subs = [(0, 16), (16, 128)] if last else [(128 * s, 128) for s in range(4)]
            nsub = len(subs)
            NSM = NSMt if last else NSMp

            def pass_subs(pi):
                return [(0, 16), (16, 128)] if pi == 4 else [(128 * s, 128) for s in range(4)]

            def load_x(pi, dst, keys_of, slot_of):
                for s, (c0, n) in enumerate(pass_subs(pi)):
                    if pi == 4:
                        if s == 0:
                            DMA('sp', dst[0:16, 0, :], xp[SEQ - 16:SEQ, :], [], keys_of(0), slot_of(0))
                        else:
                            DMA('sp', dst[:, 1, :], xs, [], keys_of(1), slot_of(1))
                    else:
                        t0 = pi * 512 + s * 128 - NMETA
                        if t0 < 0:
                            DMA('sp', dst[0:16, 0, :], meta, [], ['xT.0m'], 'xT.0m')
                            DMA('sp', dst[16:128, 0, :], xp[0:112, :], [], keys_of(0), slot_of(0))
                        else:
                            DMA('sp', dst[:, s, :], xp[t0:t0 + 128, :], [], keys_of(s), slot_of(s))

            if ps_i > 0:
                for s, (c0, n) in enumerate(subs):
                    if s % 2 == 0:
                        CP(xT[:n, s, :], XN[:n, s, :], xnk(s), ['xT.%d' % s])
                    else:
                        ACT(xT[:n, s, :], XN[:n, s, :], AF.Copy, xnk(s), ['xT.%d' % s])
                if last:
                    s0_load(5)
                    s0_load(6)

            def xk(s):
                return ['xT.%d' % s, 'xT.0m'] if s == 0 else ['xT.%d' % s]

            def stage_major(chains):
                for k in range(max(len(c) for c in chains)):
                    for c in chains:
                        if k < len(c):
                            c[k]()

            def xnk(s):
                return ['H%d.%d' % (3 + s // 2, c) for c in range(4 * (s % 2), 4 * (s % 2) + 4)]

            def chain_norm_a(s, n, src=None, skeys=None):
                if src is None:
                    src, skeys = xT, xk(s)
                sk = 'stat.%d' % s
                ss, rs = stat[:n, 4 * s:4 * s + 1], stat[:n, 4 * s + 1:4 * s + 2]
                return [
                    lambda: ACT(xnb[:n, s, :], src[:n, s, :], AF.Square, skeys, ['xnb.%d' % s, sk], accum_out=ss),
                    lambda: ACT(rs, ss, AF.Ln, [sk], [sk], scale=1.0 / D, bias=EPS),
                    lambda: ACT(rs, rs, AF.Exp, [sk], [sk], scale=-0.5),
                    lambda: TS(xnb[:n, s, :], src[:n, s, :], rs, None, ALU.mult, None, skeys + [sk], ['xnb.%d' % s]),
                ]

            def chain_norm_b(s, c0, n, gi):
                def trs():
                    for c in range(8):
                        TR(psT[:, c * 128:c * 128 + n], xnb[:n, s, c * 128:(c + 1) * 128], identb[:n, :n],
                           ['xnb.%d' % s, 'identb'], ['psT'])
                def both():
                    trs()
                    TT(H0[:, :, c0:c0 + n], psT[:, :].rearrange("p (c t) -> p c t", c=8)[:, :, 0:n],
                       pr(gi).to_broadcast([128, 8, n]), ALU.mult, ['psT', 'prm'], kk('H0'))
                return [both]

            def norm_a(s, n, src=None, skeys=None):
                run(chain_norm_a(s, n, src, skeys))

            def norm_b(s, c0, n, gi):
                run(chain_norm_b(s, c0, n, gi))

            if ps_i == 0:
                stage_major([chain_norm_a(s, n) for s, (c0, n) in enumerate(subs)])
                for s, (c0, n) in enumerate(subs):
                    norm_b(s, c0, n, P_G0)

            def projF(src_w, row0, col0, rhsbuf, rhskeys, evac, pool=(0, 1, 2, 3, 4, 5, 6)):
                st8 = {}

                def mk(blk):
                    def task():
                        if 'sl' not in st8:
                            st8['sl'], st8['key'] = load_slab(wslab(src_w, row0, col0))
                        sl, skey = st8['sl'], st8['key']
                        b = nb(pool)
                        for kc in range(8):
                            MM(banks[b][:, :T], sl[:, kc, blk * 128:(blk + 1) * 128], rhsbuf[:, kc, :T],
                               kc == 0, kc == 7, [skey] + rhskeys, [bk(b)])
                        evac(blk, b)
                    return task
                return [mk(blk) for blk in range(4)]

            def proj_group(g, evac_c, pool=(0, 1, 2, 3, 4, 5, 6)):
                tasks = []
                for j in range(2):
                    tasks += projF(w_in, 0, 1024 * g + 512 * j, H0, kk('H0'),
                                   (lambda blk, b, j=j: evac_c(j * 4 + blk, b)), pool)
                return tasks

            if ps_i == 0:
                MEMSET(uF[:, :, 0:3], 0.0, [], HUK)
            else:
                CP(uF[:, :, 0:3], ucar[:], ['ucar'], HUK)
            run(proj_group(0, lambda c, b: ACT(uF[:, c, 3:3 + T], banks[b][:, :T], AF.Copy, [bk(b)], HUK)))
            run(proj_group(1, lambda c, b: ACT(H2[:, c, :T], banks[b][:, :T], AF.Gelu_apprx_tanh, [bk(b)], ['H2.%d' % c])))
            for j in range(2):
                sl, skey = load_slab(wslab(w_in, 0, 4096 + 512 * j))
                for s, (c0, n) in enumerate(subs):
                    b = nb()
                    for kc in range(8):
                        MM(banks[b][:n, :], H0[:, kc, c0:c0 + n], sl[:, kc, :], kc == 0, kc == 7, [skey, 'H0.%d' % kc], [bk(b)])
                    ACT(vT[:n, s, j * 512:(j + 1) * 512], banks[b][:n, :], AF.Copy, [bk(b)], ['vT.%d.%d' % (s, j)])
            if last:
                DMA('sp', sin[0:48, :], sconv, [], ['io68'], 'io68s')
                DMA('sp', sin[48:64, :], sh, [], ['io68'], 'io68s')
                b = nb()
                for c in range(8):
                    TR(banks[b][:, c * 64:(c + 1) * 64], sin[0:64, c * 128:(c + 1) * 128], identf[0:64, 0:64],
                       ['io68', 'cst'], [bk(b)])
                bv = banks[b][:, :].rearrange("p (c i) -> p c i", c=8)
                CP(upS[:, :, :, 0:3], bv[:, :, 0:48].rearrange("p c (i j) -> p c i j", j=3), [bk(b)], ['upS'])
                CP(h0s[:], bv[:, :, 48:64], [bk(b)], VK2[2:])
                CP(upS[:, :, :, 3:11], uF[:, :, 19:19 + 128].rearrange("p c (i t) -> p c i t", t=8), HUK, ['upS'])
            for c in range(8):
                segs = [(uF[:, c, 0:T + 3], W[0][:, c, 0:T], T)] if not last else [(uF[:, c, 0:19], W[0][:, c, 0:16], 16)]
                for (up, dst, L) in segs:
                    TS(dst, up[:, 0:L], pr(P_CW + 0, c), pr(P_CB, c), ALU.mult, ALU.add, HUK + ['prm'], ['W0.%d' % c])
                    for k in range(1, 4):
                        STT(dst, up[:, k:k + L], pr(P_CW + k, c), dst, ALU.mult, ALU.add, HUK + ['prm', 'W0.%d' % c], ['W0.%d' % c])
                if last:
                    dst = W[0][:, c, 16:144].rearrange("p (i t) -> p i t", t=8)
                    TS(dst, upS[:, c, :, 0:8], pr(P_CW + 0, c), pr(P_CB, c), ALU.mult, ALU.add, ['upS', 'prm'], ['W0.%d' % c])
                    for k in range(1, 4):
                        STT(dst, upS[:, c, :, k:k + 8], pr(P_CW + k, c), dst, ALU.mult, ALU.add, ['upS', 'prm', 'W0.%d' % c], ['W0.%d' % c])
                CP(H1[:, c, :T], W[0][:, c, :T], ['W0.%d' % c], ['H1.%d' % c])
            if not last:
                CP(ucar[:], uF[:, :, T:T + 3], HUK, ['ucar'])
            else:
                CP(stage[:, :, 0:48].rearrange("p c (i j) -> p c i j", j=3), upS[:, :, :, 8:11], ['upS'], VK2[:3])
                CP(stage[:, :, 64:67], uF[:, :, 16:19], HUK, VK2[:3])
            for c in range(8):
                b = nb()
                MM(banks[b][:, :T], rgwb[:, c, :], H1[:, c, :T], True, True, ['rgwb', 'H1.%d' % c], [bk(b)])
                ACT(W[1][:, c, :T], banks[b][:, :T], AF.Sigmoid, [bk(b), 'prm'], ['W1.%d' % c], bias=pr(P_RB, c))
                b = nb()
                MM(banks[b][:, :T], igwb[:, c, :], H1[:, c, :T], True, True, ['igwb', 'H1.%d' % c], [bk(b)])
                ACT(W[3][:, c, :T], banks[b][:, :T], AF.Sigmoid, [bk(b), 'prm'], ['W3.%d' % c], bias=pr(P_IB, c))
            TT(W[3][:, :, :T], W[3][:, :, :T], W[0][:, :, :T], ALU.mult, kk('W3') + kk('W0'), kk('W3'))
            qtasks = proj_group(2, lambda c, b: ACT(H3[:, c, :T], banks[b][:, :T], AF.Silu, [bk(b)], ['H3.%d' % c]))
            for c in range(8):
                ACT(W[2][:, c, :T], W[1][:, c, :T], AF.Exp, ['W1.%d' % c, 'prm'], ['W2.%d' % c], scale=pr(P_CL, c))
                TT(W[1][:, c, :T], W[2][:, c, :T], W[2][:, c, :T], ALU.mult, ['W2.%d' % c], ['W1.%d' % c])
            ACT(W[1][:, :, :T], W[1][:, :, :T], AF.Ln, kk('W1'), kk('W1'), scale=-1.0, bias=1.0)
            ACT(W[1][:, :, :T], W[1][:, :, :T], AF.Exp, kk('W1'), kk('W1'), scale=0.5)
            run(qtasks)
            if ps_i == 0:
                MEMSET(W[1][:, :, 0:1], 1.0, kk('W1'), kk('W1'))
            TT(W[1][:, :, :T], W[1][:, :, :T], W[3][:, :, :T], ALU.mult, kk('W1') + kk('W3'), kk('W1'))
            def ev_f(c, b):
                ACT(W[3][:, c, :T], banks[b][:, :T], AF.Sigmoid, [bk(b)], ['W3.%d' % c])
                TS(W[3][:, c, :T], W[3][:, c, :T], pr(P_C1, c), pr(P_LB, c), ALU.mult, ALU.add, ['W3.%d' % c, 'prm'], ['W3.%d' % c])
            ftasks = proj_group(3, ev_f)
            if last:
                afirst = W[2][:, :, 16:144:8]
                bfirst = W[1][:, :, 16:144:8]
                TT(tmpS[:], afirst, h0s[:], ALU.mult, kk('W2') + VK2[2:], VK2[2:])
                TT(bfirst, bfirst, tmpS[:], ALU.add, kk('W1') + VK2[2:], kk('W1'))
                MEMSET(afirst, 0.0, kk('W2') + VK2[2:], kk('W2'))
            for c in range(8):
                SCAN(W[0][:, c, :T], W[2][:, c, :T], W[1][:, c, :T], hc[:, c:c + 1], ALU.mult, ALU.add,
                     ['W2.%d' % c, 'W1.%d' % c, 'hc'], ['W0.%d' % c])
            if not last:
                CP(hc[:], W[0][:, :, T - 1], kk('W0'), ['hc'])
            else:
                CP(stage[:, :, 48:64], W[0][:, :, 23:144:8], kk('W0'), VK2[:3])
                CP(stage[:, :, 67:68], W[0][:, :, 15:16], kk('W0'), VK2[:3])
                b0 = nb(); b1 = nb()
                for c in range(8):
                    bb = b0 if c < 4 else b1
                    TR(banks[bb][0:68, (c % 4) * 128:(c % 4 + 1) * 128], stage[:, c, 0:68], identf, VK2[:3] + ['cst'], [bk(bb)])
                CP(outT[:, 0:512], banks[b0][0:68, :], [bk(b0)], ['io68'])
                CP(outT[:, 512:1024], banks[b1][0:68, :], [bk(b1)], ['io68'])
                DMA('sp', cs, outT[0:48, :], ['io68'], [], 'o_cs')
                DMA('sp', hs, outT[48:64, :], ['io68'], [], 'o_hs')
                DMA('sp', cp, outT[64:67, :], ['io68'], [], 'o_cp')
                DMA('sp', hp, outT[67:68, :], ['io68'], [], 'o_hp')
            for c in range(8):
                TT(H2[:, c, :T], W[0][:, c, :T], H2[:, c, :T], ALU.mult, ['W0.%d' % c, 'H2.%d' % c], ['H2.%d' % c])
            run(ftasks)

            ogtasks = proj_group(5, lambda c, b: ACT(W[0][:, c, :T], banks[b][:, :T], AF.Silu, [bk(b)], ['W0.%d' % c]))
            TS(W[1][:, :, :T], W[3][:, :, :T], -1.0, 1.0, ALU.mult, ALU.add, kk('W3'), kk('W1'))
            ACT(W[2][:, :, :T], W[3][:, :, :T], AF.Ln, kk('W3'), kk('W2'))
            run(ogtasks[0:4])
            for c in range(8):
                SCAN(W[2][:, c, :T], NSM[:, :T], W[2][:, c, :T], 0.0, ALU.mult, ALU.add, ['cst', 'W2.%d' % c], ['W2.%d' % c])
            ACT(W[3][:, :, :T], W[2][:, :, :T], AF.Exp, kk('W2'), kk('W3'))
            ACT(W[2][:, :, :T], W[2][:, :, :T], AF.Exp, kk('W2'), kk('W2'), scale=-1.0)
            run(ogtasks[4:8])
            if last:
                sgA = xnb[:, 0:2, :].rearrange("p s d -> p (s d)").rearrange("p (c t) -> p c t", c=8)
                sgk = lambda c: 'xnb.%d' % (c // 4)
            else:
                sgA = xnb[:, :, :].rearrange("p s d -> p (s d)").rearrange("p (c t) -> p c t", c=8)
                sgk = lambda c: 'xnb.%d' % (c // 2)
            run(proj_group(6, lambda c, b: ACT(sgA[:, c, :T], banks[b][:, :T], AF.Sigmoid, [bk(b)], [sgk(c)])))
            if not last:
                CP(ebuf[:, :, 0:16], W[3][:, :, 31:512:32], kk('W3'), ['ebuf'])
            else:
                CP(ebuf[:, :, 0:1], W[3][:, :, 15:16], kk('W3'), ['ebuf'])
                CP(ebuf[:, :, 1:17], W[3][:, :, 23:144:8], kk('W3'), ['ebuf'])
            TT(H1[:, :, :T], H3[:, :, :T], W[3][:, :, :T], ALU.mult, kk('H3') + kk('W3'), kk('H1'))
            TT(W[1][:, :, :T], W[1][:, :, :T], W[2][:, :, :T], ALU.mult, kk('W1') + kk('W2'), kk('W1'))
            ACT(H3[:, :, :T], W[1][:, :, :T], AF.Copy, kk('W1'), kk('H3'))
            if not last:
                v4 = lambda a: a[:, :, 0:512].rearrange("p c (j t) -> p c j t", t=32)
                TT(v4(H4), v4(W[1]), v4(W[3])[:, :, :, 31:32].to_broadcast([128, 8, 16, 32]), ALU.mult,
                   kk('W1') + kk('W3'), kk('H4'))
            else:
                TT(H4[:, :, 0:16], W[1][:, :, 0:16], W[3][:, :, 15:16].to_broadcast([128, 8, 16]), ALU.mult,
                   kk('W1') + kk('W3'), kk('H4'))
                v4 = lambda a: a[:, :, 16:144].rearrange("p c (j t) -> p c j t", t=8)
                TT(v4(H4), v4(W[1]), v4(W[3])[:, :, :, 7:8].to_broadcast([128, 8, 16, 8]), ALU.mult,
                   kk('W1') + kk('W3'), kk('H4'))
            for s, (c0, n) in enumerate(subs):
                for h in range(8):
                    TR(psT[:n, h * 128:(h + 1) * 128], H4[:, h, c0:c0 + n], identb[:, :], ['H4.%d' % h, 'identb'], ['psT'])
                CP(kT[:n, s, :], psT[:n, :], ['psT'], ['kT.%d' % s])
            FP = (2, 7)
            fill = []
            fill += proj_group(7, (lambda c, b: ACT(W[3][:, c, :T], banks[b][:, :T], AF.Sigmoid, [bk(b)], ['W3.%d' % c])), FP)

            def do_fill(k):
                for _ in range(k):
                    if fill:
                        fill.pop(0)()

            OB = (0, 1)
            UB = (3, 4)
            ech = 0
            def emit_AT(s):
                c0, n = subs[s]
                msk = mask8 if (last and s == 1) else mask64
                ATm = ATms[s % 2]
                for g in range(2):
                    av = banks[2][:, :].rearrange("p (h t) -> p h t", h=4)
                    for hh in range(4):
                        h = 4 * g + hh
                        MM(av[:n, hh, :n], H3[:, h, c0:c0 + n], H1[:, h, c0:c0 + n], True, True, ['H3.%d' % h, 'H1.%d' % h], [bk(2)])
                    TT(ATm[:n, 4 * g:4 * g + 4, :n], av[:n, :, :n], msk[:n, :n].unsqueeze(1).to_broadcast([n, 4, n]), ALU.mult,
                       [bk(2), 'cst'], kk('ATm%d' % (s % 2), range(4 * g, 4 * g + 4)))

            emit_AT(0)
            for s, (c0, n) in enumerate(subs):
                samp = last and s == 1
                vkeys = ['vT.%d.0' % s, 'vT.%d.1' % s]
                ATm = ATms[s % 2]
                oP = [banks[OB[0]][:, :].rearrange("p (h t) -> p h t", h=4), banks[OB[1]][:, :].rearrange("p (h t) -> p h t", h=4)]
                uP = [banks[UB[0]][:, :].rearrange("p (h t) -> p h t", h=4), banks[UB[1]][:, :].rearrange("p (h t) -> p h t", h=4)]
                for h in range(8):
                    MM(oP[h // 4][:, h % 4, :n], vT[:n, s, h * 128:(h + 1) * 128], ATm[:n, h, :n], h % 4 == 0, False,
                       vkeys + ['ATm%d.%d' % (s % 2, h)], [bk(OB[h // 4])], skip_group_check=True)
                if s + 1 < nsub:
                    emit_AT(s + 1)
                if not samp:
                    chunks = [(0, 16)] if last else [(32 * j, 32 * j + 32) for j in range(4)]
                    UBS = ((3, 4), (5, 6))

                    def emit_U(ci):
                        a0, a1 = chunks[ci]
                        ubp = UBS[ci % 2]
                        tp = dict(tile_position=(a0, 0)) if a1 - a0 == 32 else {}
                        for h in range(8):
                            uv = banks[ubp[h // 4]][:, :].rearrange("p (h t) -> p h t", h=4)
                            MM(uv[:, h % 4, :], kT[a0:a1, s, h * 128:(h + 1) * 128], vT[a0:a1, s, h * 128:(h + 1) * 128],
                               True, True, ['kT.%d' % s] + vkeys, [bk(ubp[h // 4])], **tp)

                    emit_U(0)
                    for ci, (a0, a1) in enumerate(chunks):
                        ubp = UBS[ci % 2]
                        for h in range(8):
                            MM(oP[h // 4][:, h % 4, a0:a1], Sbf[:, h, :], H1[:, h, c0 + a0:c0 + a1], False,
                               ci == len(chunks) - 1 and h % 4 == 3,
                               ['Sbf.%d' % h, 'H1.%d' % h], [bk(OB[h // 4])], skip_group_check=True)
                        if ci + 1 < len(chunks):
                            emit_U(ci + 1)
                        for h in range(8):
                            uv = banks[ubp[h // 4]][:, :].rearrange("p (h t) -> p h t", h=4)
                            STT(Sst[:, h, :], Sst[:, h, :], ebuf[:, h, ech:ech + 1], uv[:, h % 4, :], ALU.mult, ALU.add,
                                ['Sst.%d' % h, 'ebuf', bk(ubp[h // 4])], ['Sst.%d' % h])
                            if h % 4 == 3:
                                g4 = range(h - 3, h + 1)
                                ACT(Sbf[:, h - 3:h + 1, :], Sst[:, h - 3:h + 1, :], AF.Copy, kk('Sst', g4), kk('Sbf', g4))
                        ech += 1
                        do_fill(1)
                    if last:
                        DMA('sp', Sp.rearrange("h k v -> k h v"), Sst[:], kk('Sst'), [], 'o_Sp')
                else:
                    UBS2 = ((3, 4), (5, 6))

                    def prep(i):
                        q = i % 2
                        rk = 'S0r.%d' % (i % NR)
                        ACT(S0b[q][:], S0f[i % NR][:], AF.Copy, [rk] + ['%s.h%d' % (rk, h) for h in range(8)], ['xnb.%d' % (2 + q)])
                        TS(kTm[q][:], kT[:, 1, :], rowmask[:, i:i + 1], None, ALU.mult, None, ['kT.1', 'cst'], ['kT.%d' % (2 + q)])
                        ubp = UBS2[i % 2]
                        for h in range(8):
                            uv = banks[ubp[h // 4]][:, :].rearrange("p (h t) -> p h t", h=4)
                            MM(uv[:, h % 4, :], kTm[q][:, h * 128:(h + 1) * 128], vT[:, 1, h * 128:(h + 1) * 128], True, True,
                               ['kT.%d' % (2 + q)] + vkeys, [bk(ubp[h // 4])])

                    prep(0)
                    for i in range(NS):
                        q = i % 2
                        r = i % NR
                        rk = 'S0r.%d' % r
                        ubp = UBS2[i % 2]
                        for h in range(8):
                            MM(oP[h // 4][:, h % 4, 8 * i:8 * i + 8], S0b[q][:, h, :], H1[:, h, c0 + 8 * i:c0 + 8 * i + 8], False,
                               i == NS - 1 and h % 4 == 3,
                               ['xnb.%d' % (2 + q), 'H1.%d' % h], [bk(OB[h // 4])], skip_group_check=True)
                        if i + 1 < NS:
                            prep(i + 1)
                        for h in range(8):
                            uv = banks[ubp[h // 4]][:, :].rearrange("p (h t) -> p h t", h=4)
                            STT(S0f[r][:, h, :], S0f[r][:, h, :], ebuf[:, h, 1 + i:2 + i], uv[:, h % 4, :], ALU.mult, ALU.add,
                                [rk, 'ebuf', bk(ubp[h // 4])], ['%s.h%d' % (rk, h)])
                        DMA('sp', Ss[i].rearrange("h k v -> k h v"), S0f[r][:], [rk] + ['%s.h%d' % (rk, h) for h in range(8)], [], rk)
                        if i + NR < NS:
                            s0_load(i + NR)
                        do_fill(1)
                for g in range(2):
                    ACT(W[1][:, 4 * g:4 * g + 4, c0:c0 + n], oP[g][:, :, :n], AF.Copy, [bk(OB[g])], kk('W1', range(4 * g, 4 * g + 4)))
            do_fill(len(fill))

            def merge_proj(wm, Hs, hname, wi):
                for j in range(2):
                    def ev_p(blk, b, j=j, wi=wi):
                        c = j * 4 + blk
                        TT(W[wi][:, c, :T], W[wi][:, c, :T], banks[b][:, :T], ALU.mult, ['W%d.%d' % (wi, c), bk(b)], ['W%d.%d' % (wi, c)])
                    run(projF(wm, 0, 512 * j, Hs, kk(hname), ev_p))

            for j in range(2):
                def ev_pa(blk, b, j=j):
                    c = j * 4 + blk
                    TT(W[2][:, c, :T], sgA[:, c, :T], banks[b][:, :T], ALU.mult, [sgk(c), bk(b)], ['W2.%d' % c])
                run(projF(w_a, 0, 512 * j, H2, kk('H2'), ev_pa))
            for h in range(8):
                ACT(H3[:, h, :T], W[1][:, h, :T], AF.Square, ['W1.%d' % h], ['H3.%d' % h])
            for h in range(8):
                b = nb()
                MM(banks[b][:, :T], onesb[:, :], H3[:, h, :T], True, True, ['onesb', 'H3.%d' % h], [bk(b)])
                ACT(banks[b][:, :T], banks[b][:, :T], AF.Ln, [bk(b)], [bk(b)], scale=1.0 / 128, bias=EPS)
                ACT(banks[b][:, :T], banks[b][:, :T], AF.Exp, [bk(b)], [bk(b)], scale=-0.5)
                STT(W[1][:, h, :T], W[1][:, h, :T], pr(P_GN, h), banks[b][:, :T], ALU.mult, ALU.mult,
                    ['W1.%d' % h, 'prm', bk(b)], ['W1.%d' % h])
                TT(H4[:, h, :T], W[1][:, h, :T], W[0][:, h, :T], ALU.mult, ['W1.%d' % h, 'W0.%d' % h], ['H4.%d' % h])

            for j in range(2):
                def ev_pb(blk, b, j=j):
                    c = j * 4 + blk
                    TT(W[3][:, c, :T], W[3][:, c, :T], banks[b][:, :T], ALU.mult, ['W3.%d' % c, bk(b)], ['W3.%d' % c])
                    TT(H1[:, c, :T], W[2][:, c, :T], W[3][:, c, :T], ALU.add, ['W2.%d' % c, 'W3.%d' % c], ['H1.%d' % c])
                run(projF(w_b, 0, 512 * j, H4, kk('H4'), ev_pb))

            def zk(s):
                return ['W0.%d' % (2 * s), 'W0.%d' % (2 * s + 1)]

            def chain_resid(s, n, grep, gkey, final=False):
                sk = 'stat.%d' % s
                ss, rs = stat[:n, 4 * s + 2:4 * s + 3], stat[:n, 4 * s + 3:4 * s + 4]
                if final:
                    last_op = lambda: TT(zT[:n, s, :], xT[:n, s, :], zT[:n, s, :], ALU.add, xk(s) + zk(s), zk(s))
                else:
                    last_op = lambda: TT(xT[:n, s, :], xT[:n, s, :], zT[:n, s, :], ALU.add, xk(s) + zk(s), ['xT.%d' % s])
                return [
                    lambda: ACT(xnb[:n, s, :], zT[:n, s, :], AF.Square, zk(s), ['xnb.%d' % s, sk], accum_out=ss),
                    lambda: ACT(rs, ss, AF.Ln, [sk], [sk], scale=1.0 / D, bias=EPS),
                    lambda: ACT(rs, rs, AF.Exp, [sk], [sk], scale=-0.5),
                    lambda: STT(zT[:n, s, :], zT[:n, s, :], rs, grep[:n, :], ALU.mult, ALU.mult, zk(s) + [sk, gkey], zk(s)),
                    last_op,
                ]

            wo = [load_slab(wslab(w_out, 0, 512 * ch)) for ch in range(2)]
            groups = [list(range(0, (nsub + 1) // 2)), list(range((nsub + 1) // 2, nsub))]

            def wout_mm(grp):
                for s in grp:
                    c0, n = subs[s]
                    for ch in range(2):
                        sl, skey = wo[ch]
                        b = nb()
                        for kc in range(8):
                            MM(banks[b][:n, :], H1[:, kc, c0:c0 + n], sl[:, kc, :], kc == 0, kc == 7, [skey, 'H1.%d' % kc], [bk(b)])
                        ACT(zT[:n, s, ch * 512:(ch + 1) * 512], banks[b][:n, :], AF.Copy, [bk(b)], zk(s))

            wout_mm(groups[0])
            stage_major([chain_resid(s, subs[s][1], g1rep, 'g1rep') + chain_norm_a(s, subs[s][1]) for s in groups[0]])
            wout_mm(groups[1])
            for s in groups[0]:
                norm_b(s, subs[s][0], subs[s][1], P_G2)
            stage_major([chain_resid(s, subs[s][1], g1rep, 'g1rep') + chain_norm_a(s, subs[s][1])
                         + chain_norm_b(s, subs[s][0], subs[s][1], P_G2) for s in groups[1]])

            if not last:
                load_x(ps_i + 1, XN, xnk, lambda s: 'XN.%d' % s)
            for j in range(8):
                def ev_up(blk, b, j=j):
                    fb = j * 4 + blk
                    q = fb % 2
                    ACT(sqt[q][:, :T], banks[b][:, :T], AF.Square, [bk(b)], ['W3.%d' % q])
                    STT(FFb[:, fb, :T], banks[b][:, :T], 0.0, sqt[q][:, :T], ALU.is_gt, ALU.mult, [bk(b), 'W3.%d' % q],
                        ['W%d.%d' % (1 + fb // 16, (fb % 16) // 2)])
                run(projF(w_up, 0, 512 * j, H0, kk('H0'), ev_up))
                if j == 4 and not last:
                    stage_major([chain_norm_a(s, n, XN, xnk(s)) for s, (c0, n) in enumerate(pass_subs(ps_i + 1))])
            if not last:
                for s, (c0, n) in enumerate(pass_subs(ps_i + 1)):
                    norm_b(s, c0, n, P_G0)
            for ch in range(2):
                bs = [nb() for _ in range(nsub)]
                for kg in range(4):
                    sl, skey = load_slab(wslab(w_down, 1024 * kg, 512 * ch))
                    if ps_i == 3 and 4 * ch + kg < 5:
                        s0_load(4 * ch + kg, after=[skey])
                    for s, (c0, n) in enumerate(subs):
                        for kc in range(8):
                            MM(banks[bs[s]][:n, :], FFb[:, kg * 8 + kc, c0:c0 + n], sl[:, kc, :], kg == 0 and kc == 0,
                               kg == 3 and kc == 7, [skey] + kk('W1') + kk('W2'), [bk(bs[s])])
                for s, (c0, n) in enumerate(subs):
                    ACT(zT[:n, s, ch * 512:(ch + 1) * 512], banks[bs[s]][:n, :], AF.Copy, [bk(bs[s])], zk(s))
            def out_dma(s):
                def f():
                    if last:
                        if s == 0:
                            DMA('sp', yp[SEQ - 16:SEQ, :], zT[0:16, 0, :], zk(0), [], 'yo.0')
                        else:
                            DMA('sp', ys, zT[:, 1, :], zk(1), [], 'yo.1')
                    else:
                        t0 = ps_i * 512 + s * 128 - NMETA
                        if t0 < 0:
                            DMA('sp', yp[0:112, :], zT[16:128, 0, :], zk(0), [], 'yo.0')
                        else:
                            DMA('sp', yp[t0:t0 + 128, :], zT[:, s, :], zk(s), [], 'yo.%d' % s)
                return f

            stage_major([chain_resid(s, n, g3rep, 'g3rep', final=True) + [out_dma(s)] for s, (c0, n) in enumerate(subs)])
        S.emit()
    return nc


_NC_CACHE = {}


def kernel(x_prompt, x_sample, state_conv, state_rglru, state_hgrn, meta_tokens, norm_gains, w_in,
           conv_w, conv_b, rg_w, rg_b, ig_w, ig_b, lru_lambda, hgrn_lb, hgrn_gnorm,
           w_branch_a, w_branch_b, w_out, w_up, w_down):
    f = lambda a: np.ascontiguousarray(np.asarray(a, dtype=np.float32))
    if 'nc' not in _NC_CACHE:
        _NC_CACHE['nc'] = build_nc()
    nc = _NC_CACHE['nc']
    consts = make_consts()
    shared = {
        "meta": f(meta_tokens), "gains": f(norm_gains[0]), "w_in": f(w_in[0]), "conv_w": f(conv_w[0]),
        "conv_b": f(conv_b), "rg_w": f(rg_w[0]), "rg_b": f(rg_b), "ig_w": f(ig_w[0]), "ig_b": f(ig_b),
        "lam": f(lru_lambda), "hlb": f(hgrn_lb), "gnorm": f(hgrn_gnorm), "w_a": f(w_branch_a[0]),
        "w_b": f(w_branch_b[0]), "w_out": f(w_out[0]), "w_up": f(w_up[0]), "w_down": f(w_down[0]),
        "consts": consts,
        "prows": np.ascontiguousarray(np.concatenate([
            f(norm_gains[0])[0:1], f(norm_gains[0])[2:3], f(conv_w[0]), f(conv_b), f(rg_b), f(ig_b), f(lru_lambda),
            f(hgrn_lb), f(hgrn_gnorm), np.zeros((3, D), np.float32)], axis=0)),
    }
    in_maps = []
    for b in range(8):
        m = dict(shared)
        m["xp"] = f(x_prompt[b])
        m["xs"] = f(x_sample[NS * b:NS * (b + 1)]).reshape(NS * DEC, D)
        m["sconv"] = f(state_conv[0, NS * b:NS * (b + 1)]).reshape(NS * 3, D)
        m["sh"] = f(state_rglru[0, NS * b:NS * (b + 1)])
        m["shg"] = f(state_hgrn[0, NS * b:NS * (b + 1)])
        in_maps.append(m)
    res = run_bass_kernel_spmd(nc, in_maps, core_ids=list(range(8)))
    r = res.results
    y_prompt = np.stack([r[b]["yp"] for b in range(8)], 0)
    y_sample = np.concatenate([r[b]["ys"].reshape(NS, DEC, D) for b in range(8)], 0)
    conv_p = np.stack([r[b]["cp"] for b in range(8)], 0)[None]
    h_p = np.concatenate([r[b]["hp"] for b in range(8)], 0)[None]
    S_p = np.stack([r[b]["Sp"] for b in range(8)], 0)[None]
    conv_s = np.concatenate([r[b]["cs"].reshape(NS, 3, D) for b in range(8)], 0)[None]
    h_s = np.concatenate([r[b]["hs"] for b in range(8)], 0)[None]
    S_s = np.concatenate([r[b]["Ss"] for b in range(8)], 0)[None]
    return (y_prompt.astype(np.float32), y_sample.astype(np.float32), conv_p.astype(np.float32),
            h_p.astype(np.float32), S_p.astype(np.float32), conv_s.astype(np.float32),
            h_s.astype(np.float32), S_s.astype(np.float32))
```

```python
import contextlib
import numpy as np
import concourse.bass as bass
import concourse.mybir as mybir
from concourse.bass_utils import run_bass_kernel_spmd

F32 = mybir.dt.float32
BF16 = mybir.dt.bfloat16
AF = mybir.ActivationFunctionType
ALU = mybir.AluOpType

D = 1024
KC = 8
SEQ = 2048
NMETA = 16
NS = 16
DEC = 8
EPS = 1e-6
NPRM = 20
STRICT_SAME_ENGINE = True


class Sched:
    def __init__(self, nc):
        self.nc = nc
        self.ops = []
        self.last_write = {}
        self.readers = {}
        self.dma_slot_count = {}

    def op(self, eng, fn, reads=(), writes=(), dma_slot=None):
        deps = set()
        for k in reads:
            j = self.last_write.get(k)
            if j is not None:
                deps.add((j, 'raw'))
        for k in writes:
            j = self.last_write.get(k)
            if j is not None:
                deps.add((j, 'waw'))
            for r in self.readers.get(k, ()):
                deps.add((r, 'war'))
        idx = len(self.ops)
        rec = dict(eng=eng, fn=fn, deps=deps, dma_slot=dma_slot, signaled=False, seq=None)
        if dma_slot is not None:
            c = self.dma_slot_count.get(dma_slot, 0) + 16
            self.dma_slot_count[dma_slot] = c
            rec['dma_val'] = c
        self.ops.append(rec)
        for k in reads:
            self.readers.setdefault(k, []).append(idx)
        for k in writes:
            self.last_write[k] = idx
            self.readers[k] = []
        return idx

    def emit(self, final_wait_eng='sp'):
        nc = self.nc
        ops = self.ops
        for i, o in enumerate(ops):
            nd = {}
            for (j, kind) in o['deps']:
                p = ops[j]
                if j == i:
                    continue
                if p['dma_slot'] is None and p['eng'] == o['eng'] and o['dma_slot'] is None:
                    if o['eng'] == 'pe' or (kind != 'raw' and not STRICT_SAME_ENGINE):
                        continue
                if kind == 'raw' or j not in nd:
                    nd[j] = kind
            o['deps'] = nd
            for j in nd:
                if ops[j]['dma_slot'] is None:
                    ops[j]['signaled'] = True
        engs = ['pe', 'act', 'dve', 'pool', 'sp']
        cnt = {e: 0 for e in engs}
        for o in ops:
            if o['dma_slot'] is None and o['signaled']:
                cnt[o['eng']] += 1
                o['seq'] = cnt[o['eng']]
        with contextlib.ExitStack() as st:
            sems = {e: st.enter_context(nc.semaphore('s_' + e)) for e in engs}
            dsems = {k: st.enter_context(nc.semaphore('d_' + str(k))) for k in self.dma_slot_count}
            block = st.enter_context(nc.Block())
            per_eng = {e: [] for e in engs}
            for i, o in enumerate(ops):
                per_eng[o['eng']].append(i)

            def make(e):
                def body(eng):
                    waited = {}
                    for i in per_eng[e]:
                        o = ops[i]
                        need = {}
                        for j in sorted(o['deps']):
                            p = ops[j]
                            if p['dma_slot'] is not None:
                                s, v = dsems[p['dma_slot']], p['dma_val']
                            else:
                                s, v = sems[p['eng']], p['seq']
                            if v > need.get(id(s), (None, 0))[1]:
                                need[id(s)] = (s, v)
                        for key, (s, v) in need.items():
                            if waited.get(key, 0) >= v:
                                continue
                            waited[key] = v
                            eng.wait_ge(s, v)
                        ins = o['fn'](eng)
                        if o['dma_slot'] is not None:
                            ins.then_inc(dsems[o['dma_slot']], 16)
                        elif o['signaled']:
                            ins.then_inc(sems[e], 1)
                    if e == final_wait_eng:
                        for k, c in self.dma_slot_count.items():
                            if waited.get(id(dsems[k]), 0) < c:
                                eng.wait_ge(dsems[k], c)
                return body

            block.tensor(make('pe'))
            block.scalar(make('act'))
            block.vector(make('dve'))
            block.gpsimd(make('pool'))
            block.sync(make('sp'))
        return cnt


def make_consts():
    c = np.zeros((128, 1184), np.float32)
    p = np.arange(128)
    c[:, 0:128] = np.eye(128)
    same32 = (p[:, None] // 32) == (p[None, :] // 32)
    same8 = (p[:, None] // 8) == (p[None, :] // 8)
    le = p[:, None] <= p[None, :]
    c[:, 128:256] = (same32 & le)
    c[:, 256:384] = (same8 & le)
    c[:, 384:400] = (p[:, None] // 8) == np.arange(16)[None, :]
    t = np.arange(512)
    c[:, 400:912] = (t % 32 != 0)[None, :]
    tt = np.arange(144)
    c[:, 912:1056] = (~((tt == 0) | ((tt >= 16) & ((tt - 16) % 8 == 0))))[None, :]
    c[:, 1056:1184] = 1.0
    return c


def build_nc():
    nc = bass.Bass("TRN2", target_bir_lowering=False)
    din = lambda n, s: nc.dram_tensor(n, s, F32, kind="ExternalInput").ap()
    dout = lambda n, s: nc.dram_tensor(n, s, F32, kind="ExternalOutput").ap()
    xp = din("xp", [SEQ, D]); xs = din("xs", [NS * DEC, D])
    sconv = din("sconv", [NS * 3, D]); sh = din("sh", [NS, D]); shg = din("shg", [NS, 8, 128, 128])
    meta = din("meta", [NMETA, D]); gains = din("gains", [4, D])
    w_in = din("w_in", [D, 8 * D]); conv_w = din("conv_w", [4, D]); conv_b = din("conv_b", [1, D])
    rg_w = din("rg_w", [8, 128, 128]); rg_b = din("rg_b", [1, D])
    ig_w = din("ig_w", [8, 128, 128]); ig_b = din("ig_b", [1, D])
    lam = din("lam", [1, D]); hlb = din("hlb", [2, D]); gnorm = din("gnorm", [1, D])
    w_a = din("w_a", [D, D]); w_b = din("w_b", [D, D]); w_out = din("w_out", [D, D])
    w_up = din("w_up", [D, 4 * D]); w_down = din("w_down", [4 * D, D])
    consts = din("consts", [128, 1184])
    prows = din("prows", [16, D])
    yp = dout("yp", [SEQ, D]); ys = dout("ys", [NS * DEC, D])
    cp = dout("cp", [3, D]); hp = dout("hp", [1, D]); Sp = dout("Sp", [8, 128, 128])
    cs = dout("cs", [NS * 3, D]); hs = dout("hs", [NS, D]); Ss = dout("Ss", [NS, 8, 128, 128])

    with contextlib.ExitStack() as st:
        SB = lambda name, shape, dt: st.enter_context(nc.sbuf_tensor(name, shape, dt))
        PS = lambda name, shape, dt: st.enter_context(nc.psum_tensor(name, shape, dt))
        S = Sched(nc)

        WB = SB("WB", [128, 4 * 4096], F32)
        W = [WB[:, i * 4096:(i + 1) * 4096].rearrange("p (c t) -> p c t", c=8) for i in range(4)]
        zT = WB[:, 0:4096].rearrange("p (s d) -> p s d", s=4)
        HB = [SB("H%d" % i, [128, 8, 512], BF16) for i in range(3)]
        HU = SB("HU", [128, 8, 516], F32)
        HUflat = HU[:, :, :].rearrange("p c t -> p (c t)")
        H3 = HUflat[:, 0:2048].bitcast(BF16).rearrange("p (c t) -> p c t", c=8)
        H4 = HUflat[:, 2048:4096].bitcast(BF16).rearrange("p (c t) -> p c t", c=8)
        uF = HU
        XN = HUflat[:, 0:4096].rearrange("p (s d) -> p s d", s=4)
        ucar = SB("ucar", [128, 8, 3], F32)
        FFb = WB[:, 4096:12288].bitcast(BF16).rearrange("p (f t) -> p f t", f=32)
        xT = SB("xT", [128, 4, 1024], F32)
        xnb = SB("xnb", [128, 4, 1024], BF16)
        vT = SB("vT", [128, 4, 1024], BF16)
        kT = SB("kT", [128, 4, 1024], BF16)
        slabs = [SB("slab%d" % i, [128, 8, 512], BF16) for i in range(3)]
        rgwb = SB("rgwb", [128, 8, 128], BF16)
        igwb = SB("igwb", [128, 8, 128], BF16)
        g1rep = SB("g1rep", [128, 1024], F32)
        g3rep = SB("g3rep", [128, 1024], F32)
        cst = SB("cst", [128, 1184], F32)
        identb = SB("identb", [128, 128], BF16)
        onesb = SB("onesb", [128, 128], BF16)
        io68 = SB("io68", [68, 1024], F32)
        prow = io68
        sin = io68
        outT = io68
        prm = SB("prm", [128, 8, NPRM], F32)
        hc = SB("hc", [128, 8], F32)
        ebuf = SB("ebuf", [128, 8, 17], F32)
        Sst = SB("Sst", [128, 8, 128], F32)
        Sbf = SB("Sbf", [128, 8, 128], BF16)
        ATms = [SB("ATm", [128, 8, 128], BF16), SB("ATm2", [128, 8, 128], BF16)]
        stat = SB("stat", [128, 16], F32)
        sqt = [W[3][:, i, :] for i in range(2)]
        upS = SB("upS", [128, 8, 16, 11], F32)
        v23 = vT[:, 2:4, :].rearrange("p s d -> p (s d)").bitcast(F32)
        stage = v23[:, 0:544].rearrange("p (c i) -> p c i", c=8)
        h0s = v23[:, 544:672].rearrange("p (c i) -> p c i", c=8)
        tmpS = v23[:, 672:800].rearrange("p (c i) -> p c i", c=8)
        S0f = [W[3][:, :, 256:384], W[3][:, :, 384:512],
               HB[1][:, :, 256:512].bitcast(F32), HB[2][:, :, 256:512].bitcast(F32), HB[0][:, :, 256:512].bitcast(F32),
               xT[:, 2, :].rearrange("p (h v) -> p h v", h=8), xT[:, 3, :].rearrange("p (h v) -> p h v", h=8)]
        NR = len(S0f)
        S0b = [xnb[:, 2 + i, :].rearrange("p (h v) -> p h v", h=8) for i in range(2)]
        kTm = [kT[:, 2 + i, :] for i in range(2)]
        banks = [PS("pb%d" % i, [128, 512], F32) for i in range(7)]
        psT = PS("psT", [128, 1024], BF16)
        banks.append(psT[:, :].bitcast(F32))

        identf = cst[:, 0:128]
        mask64 = cst[:, 128:256]
        mask8 = cst[:, 256:384]
        rowmask = cst[:, 384:400]
        NSMp = cst[:, 400:912]
        NSMt = cst[:, 912:1056]
        H0, H1, H2 = HB

        bank_rr = [0]

        def nb(pool=(0, 1, 2, 3, 4, 5, 6)):
            i = pool[bank_rr[0] % len(pool)]
            bank_rr[0] += 1
            return i

        def bk(i):
            return 'psT' if i == 7 else 'pb%d' % i

        def MM(out, lhsT, rhs, start, stop, reads, writes, **kw):
            S.op('pe', lambda e: e.matmul(out=out, lhsT=lhsT, rhs=rhs, start=start, stop=stop, **kw), reads, writes)

        def TR(out, in_, ident, reads, writes):
            S.op('pe', lambda e: e.transpose(out=out, in_=in_, identity=ident), reads, writes)

        def ACT(out, in_, func, reads, writes, **kw):
            S.op('act', lambda e: e.activation(out=out, in_=in_, func=func, **kw), reads, writes)

        def TT(out, in0, in1, op, reads, writes, eng='dve'):
            S.op(eng, lambda e: e.tensor_tensor(out=out, in0=in0, in1=in1, op=op), reads, writes)

        def TS(out, in0, s1, s2, op0, op1, reads, writes, eng='dve'):
            if s2 is None:
                S.op(eng, lambda e: e.tensor_scalar(out=out, in0=in0, scalar1=s1, scalar2=None, op0=op0), reads, writes)
            else:
                S.op(eng, lambda e: e.tensor_scalar(out=out, in0=in0, scalar1=s1, scalar2=s2, op0=op0, op1=op1), reads, writes)

        def STT(out, in0, scalar, in1, op0, op1, reads, writes, eng='dve'):
            S.op(eng, lambda e: e.scalar_tensor_tensor(out=out, in0=in0, scalar=scalar, in1=in1, op0=op0, op1=op1), reads, writes)

        def CP(out, in_, reads, writes, eng='dve'):
            S.op(eng, lambda e: e.tensor_copy(out=out, in_=in_), reads, writes)

        def SCAN(out, d0, d1, init, op0, op1, reads, writes):
            S.op('dve', lambda e: e.tensor_tensor_scan(out=out, data0=d0, data1=d1, initial=init, op0=op0, op1=op1), reads, writes)

        def RECIP(out, in_, reads, writes):
            S.op('dve', lambda e: e.reciprocal(out=out, in_=in_), reads, writes)

        def MEMSET(ap, val, reads, writes, eng='dve'):
            S.op(eng, lambda e: e.memset(ap, val), reads, writes)

        def DMA(q, out, in_, reads, writes, slot, **kw):
            S.op(q, lambda e: e.dma_start(out=out, in_=in_, **kw), reads, writes, dma_slot=slot)

        def kk(name, idx=range(8)):
            return ['%s.%d' % (name, i) for i in idx]

        HUK = kk('H3') + kk('H4')
        VK2 = ['vT.2.0', 'vT.2.1', 'vT.3.0', 'vT.3.1']

        DMA('sp', xT[0:16, 0, :], meta, [], ['xT.0m'], 'xT.0m')
        DMA('sp', xT[16:128, 0, :], xp[0:112, :], [], ['xT.0'], 'xT.0')
        for s_ in range(1, 4):
            DMA('sp', xT[:, s_, :], xp[128 * s_ - NMETA:128 * s_ - NMETA + 128, :], [], ['xT.%d' % s_], 'xT.%d' % s_)
        DMA('sp', cst[:], consts, [], ['cst'], 'cst')
        DMA('pool', identb[:], consts[:, 0:128], [], ['identb'], 'identb')
        DMA('pool', onesb[:], consts[:, 1056:1184], [], ['onesb'], 'onesb')
        DMA('pool', rgwb[:], rg_w.rearrange("h i j -> i h j"), [], ['rgwb'], 'rgwb')
        DMA('pool', igwb[:], ig_w.rearrange("h i j -> i h j"), [], ['igwb'], 'igwb')
        DMA('sp', g1rep[:], gains[1, :].partition_broadcast(128), [], ['g1rep'], 'g1rep')
        DMA('sp', g3rep[:], gains[3, :].partition_broadcast(128), [], ['g3rep'], 'g3rep')
        DMA('sp', prow[0:16, :], prows, [], ['io68'], 'io68')
        pbi = nb()
        for c in range(8):
            TR(banks[pbi][:, c * 16:c * 16 + 16], prow[0:16, c * 128:(c + 1) * 128], identf[0:16, 0:16],
               ['io68', 'cst'], [bk(pbi)])
        MEMSET(prm[:], 0.0, [], ['prm'])
        CP(prm[:, :, 0:16], banks[pbi][:, 0:128].rearrange("p (c i) -> p c i", c=8), [bk(pbi)], ['prm'])
        P_G0, P_G2, P_CW, P_CB, P_RB, P_IB, P_LAM, P_L0, P_L1, P_GN = 0, 1, 2, 6, 7, 8, 9, 10, 11, 12
        P_CL, P_CL2, P_LB, P_C1, P_T = 13, 14, 15, 16, 17

        def pr(i, c=None):
            if c is None:
                return prm[:, :, i:i + 1]
            return prm[:, c, i:i + 1]

        ACT(pr(P_T), pr(P_LAM), AF.Exp, ['prm'], ['prm'], scale=-1.0)
        ACT(pr(P_T), pr(P_T), AF.Ln, ['prm'], ['prm'], bias=1.0)
        TS(pr(P_CL), pr(P_T), -8.0, None, ALU.mult, None, ['prm'], ['prm'])
        TS(pr(P_CL2), pr(P_T), -16.0, None, ALU.mult, None, ['prm'], ['prm'])
        TT(pr(P_T), pr(P_L0), pr(P_L1), ALU.subtract, ['prm'], ['prm'])
        ACT(pr(P_LB), pr(P_T), AF.Sigmoid, ['prm'], ['prm'])
        TS(pr(P_C1), pr(P_LB), -1.0, 1.0, ALU.mult, ALU.add, ['prm'], ['prm'])
        MEMSET(hc[:], 0.0, [], ['hc'])
        MEMSET(Sst[:], 0.0, [], kk('Sst'))
        MEMSET(Sbf[:], 0.0, [], kk('Sbf'))

        slab_rr = [0]

        NSLAB_PASS = 38
        wscr = nc.dram_tensor("wscr", [NSLAB_PASS, 128, 8 * 512], BF16).ap()
        cur_pass = [0]
        slab_k = [0]

        def load_slab(src_ap):
            i = slab_rr[0] % 3
            slab_rr[0] += 1
            k = slab_k[0]
            slab_k[0] += 1
            key = 'slab%d' % i
            scr = wscr[k].rearrange("p (c n) -> p c n", c=8)
            if cur_pass[0] >= 1:
                DMA('pool', slabs[i][:], scr, ['scr.%d' % k], [key], key)
            else:
                DMA('pool', slabs[i][:], src_ap, [], [key], key)
                if cur_pass[0] == 0:
                    DMA('sp', scr, slabs[i][:], [key], ['scr.%d' % k], 'scr.%d' % k)
            return slabs[i], key

        def wslab(wmat, row0, col0):
            return wmat[row0:row0 + 1024, col0:col0 + 512].rearrange("(kc p) n -> p kc n", p=128)

        S0_EXTRA = [kk('W3'), kk('W3'), kk('H1'), kk('H2'), kk('H0'), ['xT.2'], ['xT.3']]
        s0_seen = set()

        def s0_load(i, after=()):
            q = i % NR
            extra = [] if q in s0_seen else S0_EXTRA[q]
            s0_seen.add(q)
            DMA('sp', S0f[q][:], shg[i].rearrange("h k v -> k h v"), list(after), ['S0r.%d' % q] + extra, 'S0r.%d' % q)

        def run(tasks):
            for t in tasks:
                t()

        for ps_i in range(5):
            last = ps_i == 4
            assert ps_i == 0 or slab_k[0] == NSLAB_PASS, slab_k[0]
            cur_pass[0] = ps_i
            slab_k[0] = 0
            T = 144 if last else 512
            subs = [(0, 16), (16, 128)] if last else [(128 * s, 128) for s in range(4)]
            nsub = len(subs)
            NSM = NSMt if last else NSMp

            def pass_subs(pi):
                return [(0, 16), (16, 128)] if pi == 4 else [(128 * s, 128) for s in range(4)]

            def load_x(pi, dst, keys_of, slot_of):
                for s, (c0, n) in enumerate(pass_subs(pi)):
                    if pi == 4:
                        if s == 0:
                            DMA('sp', dst[0:16, 0, :], xp[SEQ - 16:SEQ, :], [], keys_of(0), slot_of(0))
                        else:
                            DMA('sp', dst[:, 1, :], xs, [], keys_of(1), slot_of(1))
                    else:
                        t0 = pi * 512 + s * 128 - NMETA
                        if t0 < 0:
                            DMA('sp', dst[0:16, 0, :], meta, [], ['xT.0m'], 'xT.0m')
                            DMA('sp', dst[16:128, 0, :], xp[0:112, :], [], keys_of(0), slot_of(0))
                        else:
                            DMA('sp', dst[:, s, :], xp[t0:t0 + 128, :], [], keys_of(s), slot_of(s))

            if ps_i > 0:
                for s, (c0, n) in enumerate(subs):
                    if s % 2 == 0:
                        CP(xT[:n, s, :], XN[:n, s, :], xnk(s), ['xT.%d' % s])
                    else:
                        ACT(xT[:n, s, :], XN[:n, s, :], AF.Copy, xnk(s), ['xT.%d' % s])
                if last:
                    s0_load(5)
                    s0_load(6)

            def xk(s):
                return ['xT.%d' % s, 'xT.0m'] if s == 0 else ['xT.%d' % s]

            def stage_major(chains):
                for k in range(max(len(c) for c in chains)):
                    for c in chains:
                        if k < len(c):
                            c[k]()

            def xnk(s):
                return ['H%d.%d' % (3 + s // 2, c) for c in range(4 * (s % 2), 4 * (s % 2) + 4)]

            def chain_norm_a(s, n, src=None, skeys=None):
                if src is None:
                    src, skeys = xT, xk(s)
                sk = 'stat.%d' % s
                ss, rs = stat[:n, 4 * s:4 * s + 1], stat[:n, 4 * s + 1:4 * s + 2]
                return [
                    lambda: ACT(xnb[:n, s, :], src[:n, s, :], AF.Square, skeys, ['xnb.%d' % s, sk], accum_out=ss),
                    lambda: ACT(rs, ss, AF.Ln, [sk], [sk], scale=1.0 / D, bias=EPS),
                    lambda: ACT(rs, rs, AF.Exp, [sk], [sk], scale=-0.5),
                    lambda: TS(xnb[:n, s, :], src[:n, s, :], rs, None, ALU.mult, None, skeys + [sk], ['xnb.%d' % s]),
                ]

            def chain_norm_b(s, c0, n, gi):
                def trs():
                    for c in range(8):
                        TR(psT[:, c * 128:c * 128 + n], xnb[:n, s, c * 128:(c + 1) * 128], identb[:n, :n],
                           ['xnb.%d' % s, 'identb'], ['psT'])
                def both():
                    trs()
                    TT(H0[:, :, c0:c0 + n], psT[:, :].rearrange("p (c t) -> p c t", c=8)[:, :, 0:n],
                       pr(gi).to_broadcast([128, 8, n]), ALU.mult, ['psT', 'prm'], kk('H0'))
                return [both]

            def norm_a(s, n, src=None, skeys=None):
                run(chain_norm_a(s, n, src, skeys))

            def norm_b(s, c0, n, gi):
                run(chain_norm_b(s, c0, n, gi))

            if ps_i == 0:
                stage_major([chain_norm_a(s, n) for s, (c0, n) in enumerate(subs)])
                for s, (c0, n) in enumerate(subs):
                    norm_b(s, c0, n, P_G0)

            def projF(src_w, row0, col0, rhsbuf, rhskeys, evac, pool=(0, 1, 2, 3, 4, 5, 6)):
                st8 = {}

                def mk(blk):
                    def task():
                        if 'sl' not in st8:
                            st8['sl'], st8['key'] = load_slab(wslab(src_w, row0, col0))
                        sl, skey = st8['sl'], st8['key']
                        b = nb(pool)
                        for kc in range(8):
                            MM(banks[b][:, :T], sl[:, kc, blk * 128:(blk + 1) * 128], rhsbuf[:, kc, :T],
                               kc == 0, kc == 7, [skey] + rhskeys, [bk(b)])
                        evac(blk, b)
                    return task
                return [mk(blk) for blk in range(4)]

            def proj_group(g, evac_c, pool=(0, 1, 2, 3, 4, 5, 6)):
                tasks = []
                for j in range(2):
                    tasks += projF(w_in, 0, 1024 * g + 512 * j, H0, kk('H0'),
                                   (lambda blk, b, j=j: evac_c(j * 4 + blk, b)), pool)
                return tasks

            if ps_i == 0:
                MEMSET(uF[:, :, 0:3], 0.0, [], HUK)
            else:
                CP(uF[:, :, 0:3], ucar[:], ['ucar'], HUK)
            run(proj_group(0, lambda c, b: ACT(uF[:, c, 3:3 + T], banks[b][:, :T], AF.Copy, [bk(b)], HUK)))
            run(proj_group(1, lambda c, b: ACT(H2[:, c, :T], banks[b][:, :T], AF.Gelu_apprx_tanh, [bk(b)], ['H2.%d' % c])))
            for j in range(2):
                sl, skey = load_slab(wslab(w_in, 0, 4096 + 512 * j))
                for s, (c0, n) in enumerate(subs):
                    b = nb()
                    for kc in range(8):
                        MM(banks[b][:n, :], H0[:, kc, c0:c0 + n], sl[:, kc, :], kc == 0, kc == 7, [skey, 'H0.%d' % kc], [bk(b)])
                    ACT(vT[:n, s, j * 512:(j + 1) * 512], banks[b][:n, :], AF.Copy, [bk(b)], ['vT.%d.%d' % (s, j)])
            if last:
                DMA('sp', sin[0:48, :], sconv, [], ['io68'], 'io68s')
                DMA('sp', sin[48:64, :], sh, [], ['io68'], 'io68s')
                b = nb()
                for c in range(8):
                    TR(banks[b][:, c * 64:(c + 1) * 64], sin[0:64, c * 128:(c + 1) * 128], identf[0:64, 0:64],
                       ['io68', 'cst'], [bk(b)])
                bv = banks[b][:, :].rearrange("p (c i) -> p c i", c=8)
                CP(upS[:, :, :, 0:3], bv[:, :, 0:48].rearrange("p c (i j) -> p c i j", j=3), [bk(b)], ['upS'])
                CP(h0s[:], bv[:, :, 48:64], [bk(b)], VK2[2:])
                CP(upS[:, :, :, 3:11], uF[:, :, 19:19 + 128].rearrange("p c (i t) -> p c i t", t=8), HUK, ['upS'])
            for c in range(8):
                segs = [(uF[:, c, 0:T + 3], W[0][:, c, 0:T], T)] if not last else [(uF[:, c, 0:19], W[0][:, c, 0:16], 16)]
                for (up, dst, L) in segs:
                    TS(dst, up[:, 0:L], pr(P_CW + 0, c), pr(P_CB, c), ALU.mult, ALU.add, HUK + ['prm'], ['W0.%d' % c])
                    for k in range(1, 4):
                        STT(dst, up[:, k:k + L], pr(P_CW + k, c), dst, ALU.mult, ALU.add, HUK + ['prm', 'W0.%d' % c], ['W0.%d' % c])
                if last:
                    dst = W[0][:, c, 16:144].rearrange("p (i t) -> p i t", t=8)
                    TS(dst, upS[:, c, :, 0:8], pr(P_CW + 0, c), pr(P_CB, c), ALU.mult, ALU.add, ['upS', 'prm'], ['W0.%d' % c])
                    for k in range(1, 4):
                        STT(dst, upS[:, c, :, k:k + 8], pr(P_CW + k, c), dst, ALU.mult, ALU.add, ['upS', 'prm', 'W0.%d' % c], ['W0.%d' % c])
                CP(H1[:, c, :T], W[0][:, c, :T], ['W0.%d' % c], ['H1.%d' % c])
            if not last:
                CP(ucar[:], uF[:, :, T:T + 3], HUK, ['ucar'])
            else:
                CP(stage[:, :, 0:48].rearrange("p c (i j) -> p c i j", j=3), upS[:, :, :, 8:11], ['upS'], VK2[:3])
                CP(stage[:, :, 64:67], uF[:, :, 16:19], HUK, VK2[:3])
            for c in range(8):
                b = nb()
                MM(banks[b][:, :T], rgwb[:, c, :], H1[:, c, :T], True, True, ['rgwb', 'H1.%d' % c], [bk(b)])
                ACT(W[1][:, c, :T], banks[b][:, :T], AF.Sigmoid, [bk(b), 'prm'], ['W1.%d' % c], bias=pr(P_RB, c))
                b = nb()
                MM(banks[b][:, :T], igwb[:, c, :], H1[:, c, :T], True, True, ['igwb', 'H1.%d' % c], [bk(b)])
                ACT(W[3][:, c, :T], banks[b][:, :T], AF.Sigmoid, [bk(b), 'prm'], ['W3.%d' % c], bias=pr(P_IB, c))
            TT(W[3][:, :, :T], W[3][:, :, :T], W[0][:, :, :T], ALU.mult, kk('W3') + kk('W0'), kk('W3'))
            qtasks = proj_group(2, lambda c, b: ACT(H3[:, c, :T], banks[b][:, :T], AF.Silu, [bk(b)], ['H3.%d' % c]))
            for c in range(8):
                ACT(W[2][:, c, :T], W[1][:, c, :T], AF.Exp, ['W1.%d' % c, 'prm'], ['W2.%d' % c], scale=pr(P_CL, c))
                TT(W[1][:, c, :T], W[2][:, c, :T], W[2][:, c, :T], ALU.mult, ['W2.%d' % c], ['W1.%d' % c])
            ACT(W[1][:, :, :T], W[1][:, :, :T], AF.Ln, kk('W1'), kk('W1'), scale=-1.0, bias=1.0)
            ACT(W[1][:, :, :T], W[1][:, :, :T], AF.Exp, kk('W1'), kk('W1'), scale=0.5)
            run(qtasks)
            if ps_i == 0:
                MEMSET(W[1][:, :, 0:1], 1.0, kk('W1'), kk('W1'))
            TT(W[1][:, :, :T], W[1][:, :, :T], W[3][:, :, :T], ALU.mult, kk('W1') + kk('W3'), kk('W1'))
            def ev_f(c, b):
                ACT(W[3][:, c, :T], banks[b][:, :T], AF.Sigmoid, [bk(b)], ['W3.%d' % c])
                TS(W[3][:, c, :T], W[3][:, c, :T], pr(P_C1, c), pr(P_LB, c), ALU.mult, ALU.add, ['W3.%d' % c, 'prm'], ['W3.%d' % c])
            ftasks = proj_group(3, ev_f)
            if last:
                afirst = W[2][:, :, 16:144:8]
                bfirst = W[1][:, :, 16:144:8]
                TT(tmpS[:], afirst, h0s[:], ALU.mult, kk('W2') + VK2[2:], VK2[2:])
                TT(bfirst, bfirst, tmpS[:], ALU.add, kk('W1') + VK2[2:], kk('W1'))
                MEMSET(afirst, 0.0, kk('W2') + VK2[2:], kk('W2'))
            for c in range(8):
                SCAN(W[0][:, c, :T], W[2][:, c, :T], W[1][:, c, :T], hc[:, c:c + 1], ALU.mult, ALU.add,
                     ['W2.%d' % c, 'W1.%d' % c, 'hc'], ['W0.%d' % c])
            if not last:
                CP(hc[:], W[0][:, :, T - 1], kk('W0'), ['hc'])
            else:
                CP(stage[:, :, 48:64], W[0][:, :, 23:144:8], kk('W0'), VK2[:3])
                CP(stage[:, :, 67:68], W[0][:, :, 15:16], kk('W0'), VK2[:3])
                b0 = nb(); b1 = nb()
                for c in range(8):
                    bb = b0 if c < 4 else b1
                    TR(banks[bb][0:68, (c % 4) * 128:(c % 4 + 1) * 128], stage[:, c, 0:68], identf, VK2[:3] + ['cst'], [bk(bb)])
                CP(outT[:, 0:512], banks[b0][0:68, :], [bk(b0)], ['io68'])
                CP(outT[:, 512:1024], banks[b1][0:68, :], [bk(b1)], ['io68'])
                DMA('sp', cs, outT[0:48, :], ['io68'], [], 'o_cs')
                DMA('sp', hs, outT[48:64, :], ['io68'], [], 'o_hs')
                DMA('sp', cp, outT[64:67, :], ['io68'], [], 'o_cp')
                DMA('sp', hp, outT[67:68, :], ['io68'], [], 'o_hp')
            for c in range(8):
                TT(H2[:, c, :T], W[0][:, c, :T], H2[:, c, :T], ALU.mult, ['W0.%d' % c, 'H2.%d' % c], ['H2.%d' % c])
            run(ftasks)

            ogtasks = proj_group(5, lambda c, b: ACT(W[0][:, c, :T], banks[b][:, :T], AF.Silu, [bk(b)], ['W0.%d' % c]))
            TS(W[1][:, :, :T], W[3][:, :, :T], -1.0, 1.0, ALU.mult, ALU.add, kk('W3'), kk('W1'))
            ACT(W[2][:, :, :T], W[3][:, :, :T], AF.Ln, kk('W3'), kk('W2'))
            run(ogtasks[0:4])
            for c in range(8):
                SCAN(W[2][:, c, :T], NSM[:, :T], W[2][:, c, :T], 0.0, ALU.mult, ALU.add, ['cst', 'W2.%d' % c], ['W2.%d' % c])
            ACT(W[3][:, :, :T], W[2][:, :, :T], AF.Exp, kk('W2'), kk('W3'))
            ACT(W[2][:, :, :T], W[2][:, :, :T], AF.Exp, kk('W2'), kk('W2'), scale=-1.0)
            run(ogtasks[4:8])
            if last:
                sgA = xnb[:, 0:2, :].rearrange("p s d -> p (s d)").rearrange("p (c t) -> p c t", c=8)
                sgk = lambda c: 'xnb.%d' % (c // 4)
            else:
                sgA = xnb[:, :, :].rearrange("p s d -> p (s d)").rearrange("p (c t) -> p c t", c=8)
                sgk = lambda c: 'xnb.%d' % (c // 2)
            run(proj_group(6, lambda c, b: ACT(sgA[:, c, :T], banks[b][:, :T], AF.Sigmoid, [bk(b)], [sgk(c)])))
            if not last:
                CP(ebuf[:, :, 0:16], W[3][:, :, 31:512:32], kk('W3'), ['ebuf'])
            else:
                CP(ebuf[:, :, 0:1], W[3][:, :, 15:16], kk('W3'), ['ebuf'])
                CP(ebuf[:, :, 1:17], W[3][:, :, 23:144:8], kk('W3'), ['ebuf'])
            TT(H1[:, :, :T], H3[:, :, :T], W[3][:, :, :T], ALU.mult, kk('H3') + kk('W3'), kk('H1'))
            TT(W[1][:, :, :T], W[1][:, :, :T], W[2][:, :, :T], ALU.mult, kk('W1') + kk('W2'), kk('W1'))
            ACT(H3[:, :, :T], W[1][:, :, :T], AF.Copy, kk('W1'), kk('H3'))
            if not last:
                v4 = lambda a: a[:, :, 0:512].rearrange("p c (j t) -> p c j t", t=32)
                TT(v4(H4), v4(W[1]), v4(W[3])[:, :, :, 31:32].to_broadcast([128, 8, 16, 32]), ALU.mult,
                   kk('W1') + kk('W3'), kk('H4'))
            else:
                TT(H4[:, :, 0:16], W[1][:, :, 0:16], W[3][:, :, 15:16].to_broadcast([128, 8, 16]), ALU.mult,
                   kk('W1') + kk('W3'), kk('H4'))
                v4 = lambda a: a[:, :, 16:144].rearrange("p c (j t) -> p c j t", t=8)
                TT(v4(H4), v4(W[1]), v4(W[3])[:, :, :, 7:8].to_broadcast([128, 8, 16, 8]), ALU.mult,
                   kk('W1') + kk('W3'), kk('H4'))
            for s, (c0, n) in enumerate(subs):
                for h in range(8):
                    TR(psT[:n, h * 128:(h + 1) * 128], H4[:, h, c0:c0 + n], identb[:, :], ['H4.%d' % h, 'identb'], ['psT'])
                CP(kT[:n, s, :], psT[:n, :], ['psT'], ['kT.%d' % s])
            FP = (2, 7)
            fill = []
            fill += proj_group(7, (lambda c, b: ACT(W[3][:, c, :T], banks[b][:, :T], AF.Sigmoid, [bk(b)], ['W3.%d' % c])), FP)

            def do_fill(k):
                for _ in range(k):
                    if fill:
                        fill.pop(0)()

            OB = (0, 1)
            UB = (3, 4)
            ech = 0
            def emit_AT(s):
                c0, n = subs[s]
                msk = mask8 if (last and s == 1) else mask64
                ATm = ATms[s % 2]
                for g in range(2):
                    av = banks[2][:, :].rearrange("p (h t) -> p h t", h=4)
                    for hh in range(4):
                        h = 4 * g + hh
                        MM(av[:n, hh, :n], H3[:, h, c0:c0 + n], H1[:, h, c0:c0 + n], True, True, ['H3.%d' % h, 'H1.%d' % h], [bk(2)])
                    TT(ATm[:n, 4 * g:4 * g + 4, :n], av[:n, :, :n], msk[:n, :n].unsqueeze(1).to_broadcast([n, 4, n]), ALU.mult,
                       [bk(2), 'cst'], kk('ATm%d' % (s % 2), range(4 * g, 4 * g + 4)))

            emit_AT(0)
            for s, (c0, n) in enumerate(subs):
                samp = last and s == 1
                vkeys = ['vT.%d.0' % s, 'vT.%d.1' % s]
                ATm = ATms[s % 2]
                oP = [banks[OB[0]][:, :].rearrange("p (h t) -> p h t", h=4), banks[OB[1]][:, :].rearrange("p (h t) -> p h t", h=4)]
                uP = [banks[UB[0]][:, :].rearrange("p (h t) -> p h t", h=4), banks[UB[1]][:, :].rearrange("p (h t) -> p h t", h=4)]
                for h in range(8):
                    MM(oP[h // 4][:, h % 4, :n], vT[:n, s, h * 128:(h + 1) * 128], ATm[:n, h, :n], h % 4 == 0, False,
                       vkeys + ['ATm%d.%d' % (s % 2, h)], [bk(OB[h // 4])], skip_group_check=True)
                if s + 1 < nsub:
                    emit_AT(s + 1)
                if not samp:
                    chunks = [(0, 16)] if last else [(32 * j, 32 * j + 32) for j in range(4)]
                    UBS = ((3, 4), (5, 6))

                    def emit_U(ci):
                        a0, a1 = chunks[ci]
                        ubp = UBS[ci % 2]
                        tp = dict(tile_position=(a0, 0)) if a1 - a0 == 32 else {}
                        for h in range(8):
                            uv = banks[ubp[h // 4]][:, :].rearrange("p (h t) -> p h t", h=4)
                            MM(uv[:, h % 4, :], kT[a0:a1, s, h * 128:(h + 1) * 128], vT[a0:a1, s, h * 128:(h + 1) * 128],
                               True, True, ['kT.%d' % s] + vkeys, [bk(ubp[h // 4])], **tp)

                    emit_U(0)
                    for ci, (a0, a1) in enumerate(chunks):
                        ubp = UBS[ci % 2]
                        for h in range(8):
                            MM(oP[h // 4][:, h % 4, a0:a1], Sbf[:, h, :], H1[:, h, c0 + a0:c0 + a1], False,
                               ci == len(chunks) - 1 and h % 4 == 3,
                               ['Sbf.%d' % h, 'H1.%d' % h], [bk(OB[h // 4])], skip_group_check=True)
                        if ci + 1 < len(chunks):
                            emit_U(ci + 1)
                        for h in range(8):
                            uv = banks[ubp[h // 4]][:, :].rearrange("p (h t) -> p h t", h=4)
                            STT(Sst[:, h, :], Sst[:, h, :], ebuf[:, h, ech:ech + 1], uv[:, h % 4, :], ALU.mult, ALU.add,
                                ['Sst.%d' % h, 'ebuf', bk(ubp[h // 4])], ['Sst.%d' % h])
                            if h % 4 == 3:
                                g4 = range(h - 3, h + 1)
                                ACT(Sbf[:, h - 3:h + 1, :], Sst[:, h - 3:h + 1, :], AF.Copy, kk('Sst', g4), kk('Sbf', g4))
                        ech += 1
                        do_fill(1)
                    if last:
                        DMA('sp', Sp.rearrange("h k v -> k h v"), Sst[:], kk('Sst'), [], 'o_Sp')
                else:
                    UBS2 = ((3, 4), (5, 6))

                    def prep(i):
                        q = i % 2
                        rk = 'S0r.%d' % (i % NR)
                        ACT(S0b[q][:], S0f[i % NR][:], AF.Copy, [rk] + ['%s.h%d' % (rk, h) for h in range(8)], ['xnb.%d' % (2 + q)])
                        TS(kTm[q][:], kT[:, 1, :], rowmask[:, i:i + 1], None, ALU.mult, None, ['kT.1', 'cst'], ['kT.%d' % (2 + q)])
                        ubp = UBS2[i % 2]
                        for h in range(8):
                            uv = banks[ubp[h // 4]][:, :].rearrange("p (h t) -> p h t", h=4)
                            MM(uv[:, h % 4, :], kTm[q][:, h * 128:(h + 1) * 128], vT[:, 1, h * 128:(h + 1) * 128], True, True,
                               ['kT.%d' % (2 + q)] + vkeys, [bk(ubp[h // 4])])

                    prep(0)
                    for i in range(NS):
                        q = i % 2
                        r = i % NR
                        rk = 'S0r.%d' % r
                        ubp = UBS2[i % 2]
                        for h in range(8):
                            MM(oP[h // 4][:, h % 4, 8 * i:8 * i + 8], S0b[q][:, h, :], H1[:, h, c0 + 8 * i:c0 + 8 * i + 8], False,
                               i == NS - 1 and h % 4 == 3,
                               ['xnb.%d' % (2 + q), 'H1.%d' % h], [bk(OB[h // 4])], skip_group_check=True)
                        if i + 1 < NS:
                            prep(i + 1)
                        for h in range(8):
                            uv = banks[ubp[h // 4]][:, :].rearrange("p (h t) -> p h t", h=4)
                            STT(S0f[r][:, h, :], S0f[r][:, h, :], ebuf[:, h, 1 + i:2 + i], uv[:, h % 4, :], ALU.mult, ALU.add,
                                [rk, 'ebuf', bk(ubp[h // 4])], ['%s.h%d' % (rk, h)])
                        DMA('sp', Ss[i].rearrange("h k v -> k h v"), S0f[r][:], [rk] + ['%s.h%d' % (rk, h) for h in range(8)], [], rk)
                        if i + NR < NS:
                            s0_load(i + NR)
                        do_fill(1)
                for g in range(2):
                    ACT(W[1][:, 4 * g:4 * g + 4, c0:c0 + n], oP[g][:, :, :n], AF.Copy, [bk(OB[g])], kk('W1', range(4 * g, 4 * g + 4)))
            do_fill(len(fill))

            def merge_proj(wm, Hs, hname, wi):
                for j in range(2):
                    def ev_p(blk, b, j=j, wi=wi):
                        c = j * 4 + blk
                        TT(W[wi][:, c, :T], W[wi][:, c, :T], banks[b][:, :T], ALU.mult, ['W%d.%d' % (wi, c), bk(b)], ['W%d.%d' % (wi, c)])
                    run(projF(wm, 0, 512 * j, Hs, kk(hname), ev_p))

            for j in range(2):
                def ev_pa(blk, b, j=j):
                    c = j * 4 + blk
                    TT(W[2][:, c, :T], sgA[:, c, :T], banks[b][:, :T], ALU.mult, [sgk(c), bk(b)], ['W2.%d' % c])
                run(projF(w_a, 0, 512 * j, H2, kk('H2'), ev_pa))
            for h in range(8):
                ACT(H3[:, h, :T], W[1][:, h, :T], AF.Square, ['W1.%d' % h], ['H3.%d' % h])
            for h in range(8):
                b = nb()
                MM(banks[b][:, :T], onesb[:, :], H3[:, h, :T], True, True, ['onesb', 'H3.%d' % h], [bk(b)])
                ACT(banks[b][:, :T], banks[b][:, :T], AF.Ln, [bk(b)], [bk(b)], scale=1.0 / 128, bias=EPS)
                ACT(banks[b][:, :T], banks[b][:, :T], AF.Exp, [bk(b)], [bk(b)], scale=-0.5)
                STT(W[1][:, h, :T], W[1][:, h, :T], pr(P_GN, h), banks[b][:, :T], ALU.mult, ALU.mult,
                    ['W1.%d' % h, 'prm', bk(b)], ['W1.%d' % h])
                TT(H4[:, h, :T], W[1][:, h, :T], W[0][:, h, :T], ALU.mult, ['W1.%d' % h, 'W0.%d' % h], ['H4.%d' % h])

            for j in range(2):
                def ev_pb(blk, b, j=j):
                    c = j * 4 + blk
                    TT(W[3][:, c, :T], W[3][:, c, :T], banks[b][:, :T], ALU.mult, ['W3.%d' % c, bk(b)], ['W3.%d' % c])
                    TT(H1[:, c, :T], W[2][:, c, :T], W[3][:, c, :T], ALU.add, ['W2.%d' % c, 'W3.%d' % c], ['H1.%d' % c])
                run(projF(w_b, 0, 512 * j, H4, kk('H4'), ev_pb))

            def zk(s):
                return ['W0.%d' % (2 * s), 'W0.%d' % (2 * s + 1)]

            def chain_resid(s, n, grep, gkey, final=False):
                sk = 'stat.%d' % s
                ss, rs = stat[:n, 4 * s + 2:4 * s + 3], stat[:n, 4 * s + 3:4 * s + 4]
                if final:
                    last_op = lambda: TT(zT[:n, s, :], xT[:n, s, :], zT[:n, s, :], ALU.add, xk(s) + zk(s), zk(s))
                else:
                    last_op = lambda: TT(xT[:n, s, :], xT[:n, s, :], zT[:n, s, :], ALU.add, xk(s) + zk(s), ['xT.%d' % s])
                return [
                    lambda: ACT(xnb[:n, s, :], zT[:n, s, :], AF.Square, zk(s), ['xnb.%d' % s, sk], accum_out=ss),
                    lambda: ACT(rs, ss, AF.Ln, [sk], [sk], scale=1.0 / D, bias=EPS),
                    lambda: ACT(rs, rs, AF.Exp, [sk], [sk], scale=-0.5),
                    lambda: STT(zT[:n, s, :], zT[:n, s, :], rs, grep[:n, :], ALU.mult, ALU.mult, zk(s) + [sk, gkey], zk(s)),
                    last_op,
                ]

            wo = [load_slab(wslab(w_out, 0, 512 * ch)) for ch in range(2)]
            groups = [list(range(0, (nsub + 1) // 2)), list(range((nsub + 1) // 2, nsub))]

            def wout_mm(grp):
                for s in grp:
                    c0, n = subs[s]
                    for ch in range(2):
                        sl, skey = wo[ch]
                        b = nb()
                        for kc in range(8):
                            MM(banks[b][:n, :], H1[:, kc, c0:c0 + n], sl[:, kc, :], kc == 0, kc == 7, [skey, 'H1.%d' % kc], [bk(b)])
                        ACT(zT[:n, s, ch * 512:(ch + 1) * 512], banks[b][:n, :], AF.Copy, [bk(b)], zk(s))

            wout_mm(groups[0])
            stage_major([chain_resid(s, subs[s][1], g1rep, 'g1rep') + chain_norm_a(s, subs[s][1]) for s in groups[0]])
            wout_mm(groups[1])
            for s in groups[0]:
                norm_b(s, subs[s][0], subs[s][1], P_G2)
            stage_major([chain_resid(s, subs[s][1], g1rep, 'g1rep') + chain_norm_a(s, subs[s][1])
                         + chain_norm_b(s, subs[s][0], subs[s][1], P_G2) for s in groups[1]])

            if not last:
                load_x(ps_i + 1, XN, xnk, lambda s: 'XN.%d' % s)
            for j in range(8):
                def ev_up(blk, b, j=j):
                    fb = j * 4 + blk
                    q = fb % 2
                    ACT(sqt[q][:, :T], banks[b][:, :T], AF.Square, [bk(b)], ['W3.%d' % q])
                    STT(FFb[:, fb, :T], banks[b][:, :T], 0.0, sqt[q][:, :T], ALU.is_gt, ALU.mult, [bk(b), 'W3.%d' % q],
                        ['W%d.%d' % (1 + fb // 16, (fb % 16) // 2)])
                run(projF(w_up, 0, 512 * j, H0, kk('H0'), ev_up))
                if j == 4 and not last:
                    stage_major([chain_norm_a(s, n, XN, xnk(s)) for s, (c0, n) in enumerate(pass_subs(ps_i + 1))])
            if not last:
                for s, (c0, n) in enumerate(pass_subs(ps_i + 1)):
                    norm_b(s, c0, n, P_G0)
            for ch in range(2):
                bs = [nb() for _ in range(nsub)]
                for kg in range(4):
                    sl, skey = load_slab(wslab(w_down, 1024 * kg, 512 * ch))
                    if ps_i == 3 and 4 * ch + kg < 5:
                        s0_load(4 * ch + kg, after=[skey])
                    for s, (c0, n) in enumerate(subs):
                        for kc in range(8):
                            MM(banks[bs[s]][:n, :], FFb[:, kg * 8 + kc, c0:c0 + n], sl[:, kc, :], kg == 0 and kc == 0,
                               kg == 3 and kc == 7, [skey] + kk('W1') + kk('W2'), [bk(bs[s])])
                for s, (c0, n) in enumerate(subs):
                    ACT(zT[:n, s, ch * 512:(ch + 1) * 512], banks[bs[s]][:n, :], AF.Copy, [bk(bs[s])], zk(s))
            def out_dma(s):
                def f():
                    if last:
                        if s == 0:
                            DMA('sp', yp[SEQ - 16:SEQ, :], zT[0:16, 0, :], zk(0), [], 'yo.0')
                        else:
                            DMA('sp', ys, zT[:, 1, :], zk(1), [], 'yo.1')
                    else:
                        t0 = ps_i * 512 + s * 128 - NMETA
                        if t0 < 0:
                            DMA('sp', yp[0:112, :], zT[16:128, 0, :], zk(0), [], 'yo.0')
                        else:
                            DMA('sp', yp[t0:t0 + 128, :], zT[:, s, :], zk(s), [], 'yo.%d' % s)
                return f

            stage_major([chain_resid(s, n, g3rep, 'g3rep', final=True) + [out_dma(s)] for s, (c0, n) in enumerate(subs)])
        S.emit()
    return nc


_NC_CACHE = {}


def kernel(x_prompt, x_sample, state_conv, state_rglru, state_hgrn, meta_tokens, norm_gains, w_in,
           conv_w, conv_b, rg_w, rg_b, ig_w, ig_b, lru_lambda, hgrn_lb, hgrn_gnorm,
           w_branch_a, w_branch_b, w_out, w_up, w_down):
    f = lambda a: np.ascontiguousarray(np.asarray(a, dtype=np.float32))
    if 'nc' not in _NC_CACHE:
        _NC_CACHE['nc'] = build_nc()
    nc = _NC_CACHE['nc']
    consts = make_consts()
    shared = {
        "meta": f(meta_tokens), "gains": f(norm_gains[0]), "w_in": f(w_in[0]), "conv_w": f(conv_w[0]),
        "conv_b": f(conv_b), "rg_w": f(rg_w[0]), "rg_b": f(rg_b), "ig_w": f(ig_w[0]), "ig_b": f(ig_b),
        "lam": f(lru_lambda), "hlb": f(hgrn_lb), "gnorm": f(hgrn_gnorm), "w_a": f(w_branch_a[0]),
        "w_b": f(w_branch_b[0]), "w_out": f(w_out[0]), "w_up": f(w_up[0]), "w_down": f(w_down[0]),
        "consts": consts,
        "prows": np.ascontiguousarray(np.concatenate([
            f(norm_gains[0])[0:1], f(norm_gains[0])[2:3], f(conv_w[0]), f(conv_b), f(rg_b), f(ig_b), f(lru_lambda),
            f(hgrn_lb), f(hgrn_gnorm), np.zeros((3, D), np.float32)], axis=0)),
    }
    in_maps = []
    for b in range(8):
        m = dict(shared)
        m["xp"] = f(x_prompt[b])
        m["xs"] = f(x_sample[NS * b:NS * (b + 1)]).reshape(NS * DEC, D)
        m["sconv"] = f(state_conv[0, NS * b:NS * (b + 1)]).reshape(NS * 3, D)
        m["sh"] = f(state_rglru[0, NS * b:NS * (b + 1)])
        m["shg"] = f(state_hgrn[0, NS * b:NS * (b + 1)])
        in_maps.append(m)
    res = run_bass_kernel_spmd(nc, in_maps, core_ids=list(range(8)))
    r = res.results
    y_prompt = np.stack([r[b]["yp"] for b in range(8)], 0)
    y_sample = np.concatenate([r[b]["ys"].reshape(NS, DEC, D) for b in range(8)], 0)
    conv_p = np.stack([r[b]["cp"] for b in range(8)], 0)[None]
    h_p = np.concatenate([r[b]["hp"] for b in range(8)], 0)[None]
    S_p = np.stack([r[b]["Sp"] for b in range(8)], 0)[None]
    conv_s = np.concatenate([r[b]["cs"].reshape(NS, 3, D) for b in range(8)], 0)[None]
    h_s = np.concatenate([r[b]["hs"] for b in range(8)], 0)[None]
    S_s = np.concatenate([r[b]["Ss"] for b in range(8)], 0)[None]
    return (y_prompt.astype(np.float32), y_sample.astype(np.float32), conv_p.astype(np.float32),
            h_p.astype(np.float32), S_p.astype(np.float32), conv_s.astype(np.float32),
            h_s.astype(np.float32), S_s.astype(np.float32))
```
